# Optimizing a Trainium2 kernel written in Bass

```python
import math
import jax, jax.numpy as jnp
from jax import lax
import numpy as np

D_MODEL = 2048
BATCH = 8
SEQ = 2048
DEPTH = 1
DEC_BATCH = 16
DEC_SEQ = 2048
PAST_LEN = 128

D_MIX = 2048
ATTN_HEADS = 8
ATTN_KV_HEADS = 2
ATTN_HEAD_DIM = 128
ATTN_WIDTH = ATTN_HEADS * ATTN_HEAD_DIM
KV_WIDTH = ATTN_KV_HEADS * ATTN_HEAD_DIM
MLSTM_HEADS = 4
MLSTM_QK_DIM = 128
MLSTM_V_DIM = 256
MLSTM_QK_WIDTH = MLSTM_HEADS * MLSTM_QK_DIM
MLSTM_WIDTH = MLSTM_HEADS * MLSTM_V_DIM
N_GATE_COLS = 4 * MLSTM_HEADS
GRID_W = 64
ROPE_THETA = 10000.0
Q_BLOCK = 128
CHUNK = 128
EPS = 1e-6
IN_SPLITS = (ATTN_WIDTH, KV_WIDTH, KV_WIDTH, ATTN_WIDTH,
             MLSTM_QK_WIDTH, MLSTM_QK_WIDTH, MLSTM_WIDTH, MLSTM_WIDTH, MLSTM_WIDTH,
             N_GATE_COLS)
D_IN_PROJ = ATTN_WIDTH + 2 * KV_WIDTH + ATTN_WIDTH + 2 * MLSTM_QK_WIDTH + 3 * MLSTM_WIDTH + N_GATE_COLS

kernel_name = "hymba_attn_mlstm_bidir_encoder"


def rmsnorm(x, g):
    xf = x.astype(jnp.float32)
    y = xf * lax.rsqrt(jnp.mean(xf * xf, axis=-1, keepdims=True) + EPS) * g.astype(jnp.float32)
    return y.astype(x.dtype)


def axial_rope_tables(S):
    rows = S // GRID_W
    row = jnp.repeat(jnp.arange(rows), GRID_W).astype(jnp.float32)
    col = jnp.tile(jnp.arange(GRID_W), rows).astype(jnp.float32)
    nf = ATTN_HEAD_DIM // 4
    inv = 1.0 / (ROPE_THETA ** (jnp.arange(nf, dtype=jnp.float32) / nf))
    ang_r = row[:, None] * inv
    ang_c = col[:, None] * inv
    ang = jnp.concatenate([ang_r, ang_r, ang_c, ang_c], axis=-1)
    return jnp.cos(ang), jnp.sin(ang)


def apply_axial_rope(x, cos, sin):
    q = ATTN_HEAD_DIM // 4
    x1, x2, x3, x4 = x[..., :q], x[..., q:2 * q], x[..., 2 * q:3 * q], x[..., 3 * q:]
    rot = jnp.concatenate([-x2, x1, -x4, x3], axis=-1)
    return x * cos[None, :, None, :] + rot * sin[None, :, None, :]


def blocked_gqa_attention(q, k, v):
    B, S, H, D = q.shape
    G = H // ATTN_KV_HEADS
    nblk = S // Q_BLOCK
    scale = 1.0 / math.sqrt(D)
    qb = q.reshape(B, nblk, Q_BLOCK, ATTN_KV_HEADS, G, D).transpose(1, 0, 2, 3, 4, 5)

    def one_block(qi):
        s = jnp.einsum('bqkgd,bskd->bkgqs', qi, k) * scale
        p = jax.nn.softmax(s, axis=-1)
        return jnp.einsum('bkgqs,bskd->bqkgd', p, v)

    o = lax.map(one_block, qb)
    return o.transpose(1, 0, 2, 3, 4, 5).reshape(B, S, H * D)


def mlstm_chunkwise(q, k, v, i_pre, logf):
    B, H, S, dk = q.shape
    dv = v.shape[-1]
    nc = S // CHUNK

    def to_chunks(a):
        return jnp.moveaxis(a.reshape(a.shape[:2] + (nc, CHUNK) + a.shape[3:]), 2, 0)

    qc, kc, vc, ic, fc = (to_chunks(a) for a in (q, k, v, i_pre, logf))
    mask = jnp.tril(jnp.ones((CHUNK, CHUNK), dtype=bool))

    def step(carry, inp):
        C, n, m = carry
        qq, kk, vv, ii, ff = inp
        b = jnp.cumsum(ff, axis=-1)
        g = b[..., -1]
        a = b[..., :, None] - b[..., None, :] + ii[..., None, :]
        a = jnp.where(mask, a, -jnp.inf)
        inter = b + m[..., None]
        m_t = jnp.maximum(inter, jnp.max(a, axis=-1))
        w = jnp.exp(a - m_t[..., None])
        si = jnp.exp(inter - m_t)
        qk = jnp.einsum('bhtd,bhsd->bhts', qq, kk) * w
        num = jnp.einsum('bhts,bhsv->bhtv', qk, vv) + si[..., None] * jnp.einsum('bhtd,bhdv->bhtv', qq, C)
        den = jnp.sum(qk, axis=-1) + si * jnp.einsum('bhtd,bhd->bht', qq, n)
        h = num / jnp.maximum(jnp.abs(den), jnp.exp(-m_t))[..., None]
        e = g[..., None] - b + ii
        m_new = jnp.maximum(g + m, jnp.max(e, axis=-1))
        sc = jnp.exp(g + m - m_new)
        we = jnp.exp(e - m_new[..., None])
        C_new = sc[..., None, None] * C + jnp.einsum('bhs,bhsd,bhsv->bhdv', we, kk, vv)
        n_new = sc[..., None] * n + jnp.einsum('bhs,bhsd->bhd', we, kk)
        return (C_new, n_new, m_new), h

    init = (jnp.zeros((B, H, dk, dv), jnp.float32), jnp.zeros((B, H, dk), jnp.float32),
            jnp.zeros((B, H), jnp.float32))
    _, hs = lax.scan(step, init, (qc, kc, vc, ic, fc))
    return jnp.moveaxis(hs, 0, 2).reshape(B, H, S, dv)


def hybrid_layer(x, norm_g, w_in, b_gates, q_norm_g, k_norm_g, mlstm_norm_g, w_out):
    B, S, _ = x.shape
    f32 = jnp.float32
    h = rmsnorm(x, norm_g)
    proj = h @ w_in
    parts = []
    off = 0
    for size in IN_SPLITS:
        parts.append(proj[..., off:off + size])
        off += size
    aq, ak, av, az, mq, mk, mv, mo, mz, gates = parts

    cos, sin = axial_rope_tables(S)
    aq = rmsnorm(aq.reshape(B, S, ATTN_HEADS, ATTN_HEAD_DIM), q_norm_g).astype(f32)
    ak = rmsnorm(ak.reshape(B, S, ATTN_KV_HEADS, ATTN_HEAD_DIM), k_norm_g).astype(f32)
    av = av.reshape(B, S, ATTN_KV_HEADS, ATTN_HEAD_DIM).astype(f32)
    aq = apply_axial_rope(aq, cos, sin)
    ak = apply_axial_rope(ak, cos, sin)
    attn_out = blocked_gqa_attention(aq, ak, av).astype(x.dtype) * jax.nn.silu(az)

    g4 = (gates.astype(f32) + b_gates.astype(f32)).reshape(B, S, 4, MLSTM_HEADS)
    g4 = jnp.transpose(g4, (2, 0, 3, 1))
    i_f, f_f, i_b, f_b = g4[0], g4[1], g4[2], g4[3]
    mq = mq.astype(f32).reshape(B, S, MLSTM_HEADS, MLSTM_QK_DIM).transpose(0, 2, 1, 3) * (MLSTM_QK_DIM ** -0.5)
    mk = mk.astype(f32).reshape(B, S, MLSTM_HEADS, MLSTM_QK_DIM).transpose(0, 2, 1, 3)
    mv = mv.astype(f32).reshape(B, S, MLSTM_HEADS, MLSTM_V_DIM).transpose(0, 2, 1, 3)
    h_fwd = mlstm_chunkwise(mq, mk, mv, i_f, jax.nn.log_sigmoid(f_f))
    fl = lambda a: jnp.flip(a, axis=2)
    h_bwd = fl(mlstm_chunkwise(fl(mq), fl(mk), fl(mv), fl(i_b), fl(jax.nn.log_sigmoid(f_b))))
    hm = (h_fwd + h_bwd).transpose(0, 2, 1, 3)
    hm = jax.nn.sigmoid(mo.astype(f32)).reshape(B, S, MLSTM_HEADS, MLSTM_V_DIM) * hm
    hm = rmsnorm(hm, mlstm_norm_g.reshape(MLSTM_HEADS, MLSTM_V_DIM))
    mlstm_out = hm.reshape(B, S, MLSTM_WIDTH).astype(x.dtype) * jax.nn.silu(mz)

    y = jnp.concatenate([attn_out, mlstm_out], axis=-1) @ w_out
    return x + y.astype(x.dtype)


def trunk(x, norm_g, w_in, b_gates, q_norm_g, k_norm_g, mlstm_norm_g, w_out):
    for l in range(DEPTH):
        x = hybrid_layer(x, norm_g[l], w_in[l], b_gates[l], q_norm_g[l], k_norm_g[l],
                         mlstm_norm_g[l], w_out[l])
    return x


def setup_inputs(seed: int = 0) -> dict:
    key = jax.random.key(seed)
    ks = jax.random.split(key, 9)
    x_prompt = jax.random.normal(ks[0], (BATCH, SEQ, D_MODEL), jnp.float32)
    x_sample = jax.random.normal(ks[1], (DEC_BATCH, DEC_SEQ, D_MODEL), jnp.float32)
    norm_g = 1.0 + 0.02 * jax.random.normal(ks[2], (DEPTH, D_MODEL), jnp.float32)
    w_in = jax.random.normal(ks[3], (DEPTH, D_MODEL, D_IN_PROJ), jnp.float32) * (D_MODEL ** -0.5)
    fb = jnp.linspace(3.0, 6.0, MLSTM_HEADS, dtype=jnp.float32)
    zb = jnp.zeros((MLSTM_HEADS,), jnp.float32)
    gate_base = jnp.concatenate([zb, fb, zb, fb])
    b_gates = gate_base[None, :] + 0.1 * jax.random.normal(ks[4], (DEPTH, N_GATE_COLS), jnp.float32)
    q_norm_g = 1.0 + 0.02 * jax.random.normal(ks[5], (DEPTH, ATTN_HEAD_DIM), jnp.float32)
    k_norm_g = 1.0 + 0.02 * jax.random.normal(ks[6], (DEPTH, ATTN_HEAD_DIM), jnp.float32)
    mlstm_norm_g = 1.0 + 0.02 * jax.random.normal(ks[7], (DEPTH, MLSTM_WIDTH), jnp.float32)
    w_out = jax.random.normal(ks[8], (DEPTH, D_MIX, D_MODEL), jnp.float32) * (D_MIX ** -0.5)
    return {"x_prompt": x_prompt, "x_sample": x_sample, "norm_g": norm_g, "w_in": w_in,
            "b_gates": b_gates, "q_norm_g": q_norm_g, "k_norm_g": k_norm_g,
            "mlstm_norm_g": mlstm_norm_g, "w_out": w_out}


def reference(x_prompt, x_sample, norm_g, w_in, b_gates, q_norm_g, k_norm_g, mlstm_norm_g, w_out):
    y_prompt = trunk(x_prompt, norm_g, w_in, b_gates, q_norm_g, k_norm_g, mlstm_norm_g, w_out)
    y_sample = trunk(x_sample, norm_g, w_in, b_gates, q_norm_g, k_norm_g, mlstm_norm_g, w_out)
    return (y_prompt, y_sample)
```

```python
import numpy as np
from contextlib import ExitStack
import ml_dtypes
import concourse.bass as bass
import concourse.mybir as mybir
from concourse.bass_utils import run_bass_kernel_spmd

F32 = mybir.dt.float32
BF16 = mybir.dt.bfloat16
ALU = mybir.AluOpType
AF = mybir.ActivationFunctionType
AX = mybir.AxisListType


class Sched:
    EPOCH = 30000

    def __init__(self, nc, es):
        self.nc = nc
        self.es = es
        self.E = {'pe': nc.tensor, 'act': nc.scalar, 'dve': nc.vector, 'pool': nc.gpsimd, 'sp': nc.sync}
        self.sems = {}
        self.cnt = {}
        self.epoch = {k: 0 for k in self.E}
        self.known = {k: {} for k in self.E}
        self.snap = {}
        self.lastw = {}
        self.readers = {}
        self.nwaits = 0
        self.nops = 0
        self.deferred = None

    def sem(self, name):
        if name not in self.sems:
            self.sems[name] = self.es.enter_context(self.nc.semaphore(name))
            self.cnt[name] = 0
        return self.sems[name]

    def sb(self, name, shape, dt):
        return self.es.enter_context(self.nc.sbuf_tensor(name, shape, dt))

    def ps(self, name, shape, dt):
        return self.es.enter_context(self.nc.psum_tensor(name, shape, dt))

    @staticmethod
    def _bankify(reads, writes):
        rb = [r for r in reads if len(r) == 2 and r[0] == 'B' and r[1].isdigit()]
        if not rb:
            return reads, writes
        return [r for r in reads if r not in rb], list(writes) + [r for r in rb if r not in writes]

    def _deps(self, reads, writes):
        need = {}
        for r in reads:
            t = self.lastw.get(r)
            if t is not None:
                need[t[0]] = max(need.get(t[0], 0), t[1])
        for w in writes:
            t = self.lastw.get(w)
            if t is not None:
                need[t[0]] = max(need.get(t[0], 0), t[1])
            for (s, v) in self.readers.get(w, {}).items():
                need[s] = max(need.get(s, 0), v)
        return need

    def _wait(self, eng, need):
        kn = self.known[eng]
        changed = False
        for s, v in need.items():
            if kn.get(s, 0) >= v:
                continue
            if eng == 'pe' and s.startswith('pe_e'):
                continue
            self.E[eng].wait_ge(self.sems[s], v)
            self.nwaits += 1
            if not changed:
                kn = dict(kn)
                changed = True
            kn[s] = v
            sn = self.snap.get((s, v))
            if sn is not None:
                for s2, v2 in sn.items():
                    if kn.get(s2, 0) < v2:
                        kn[s2] = v2
        if changed:
            self.known[eng] = kn

    def _commit(self, ticket, reads, writes):
        for w in writes:
            self.lastw[w] = ticket
            self.readers[w] = {}
        for r in reads:
            if r in writes:
                continue
            d = self.readers.setdefault(r, {})
            d[ticket[0]] = max(d.get(ticket[0], 0), ticket[1])

    def op(self, eng, fn, reads=(), writes=()):
        if self.deferred is not None:
            self.deferred.append((eng, fn, tuple(reads), tuple(writes)))
            return None
        reads, writes = self._bankify(reads, writes)
        self._wait(eng, self._deps(reads, writes))
        sname = "%s_e%d" % (eng, self.epoch[eng])
        sem = self.sem(sname)
        ins = fn()
        ins.then_inc(sem, 1)
        self.cnt[sname] += 1
        ticket = (sname, self.cnt[sname])
        self.snap[ticket] = self.known[eng]
        if self.cnt[sname] >= self.EPOCH:
            self.epoch[eng] += 1
        self._commit(ticket, reads, writes)
        self.nops += 1
        return ticket

    def group(self, eng, fns, reads=(), writes=()):
        reads, writes = self._bankify(reads, writes)
        self._wait(eng, self._deps(reads, writes))
        sname = "%s_e%d" % (eng, self.epoch[eng])
        sem = self.sem(sname)
        ins = None
        for fn in fns:
            ins = fn()
        ins.then_inc(sem, 1)
        self.cnt[sname] += 1
        ticket = (sname, self.cnt[sname])
        self.snap[ticket] = self.known[eng]
        if self.cnt[sname] >= self.EPOCH:
            self.epoch[eng] += 1
        self._commit(ticket, reads, writes)
        self.nops += len(fns)
        return ticket

    def fence(self, eng, fn, names):
        return self.op(eng, fn, reads=(), writes=list(names))

    def barrier(self):
        need = {s: c for s, c in self.cnt.items() if c > 0}
        for e in self.E:
            self._wait(e, need)

    def dma(self, eng, out, in_, reads=(), writes=(), sem='dma', **kw):
        return self.dmas(eng, [(out, in_)], reads, writes, sem, **kw)

    def dmas(self, eng, pairs, reads=(), writes=(), sem='dma', **kw):
        self._wait(eng, self._deps(reads, writes))
        s = self.sem(sem)
        for (o, i) in pairs:
            self.E[eng].dma_start(out=o, in_=i, **kw).then_inc(s, 16)
            self.cnt[sem] += 16
        ticket = (sem, self.cnt[sem])
        self.snap[ticket] = self.known[eng]
        self._commit(ticket, reads, writes)
        self.nops += len(pairs)
        return ticket

    def finish(self):
        need = {s: c for s, c in self.cnt.items() if c > 0}
        self._wait('sp', need)


D = 2048
SEQ = 2048
NT = SEQ // 128
KC = D // 128
DIN = 6672
C_AQ, C_AK, C_AV, C_AZ = 0, 1024, 1280, 1536
C_MQ, C_MK, C_MV, C_MO, C_MZ, C_G = 2560, 3072, 3584, 4608, 5632, 6656
EPS = 1e-6
M0 = -30000.0
N_CORES = 8
SEQ_PER_CORE = 3

ARENA_BYTES = 44160


def host_consts():
    bf = ml_dtypes.bfloat16
    c = {}
    i = np.arange(128)
    c["c_ident_bf"] = np.eye(128, dtype=np.float32).astype(bf)
    c["c_ones_bf"] = np.ones((128, 128), np.float32).astype(bf)
    c["c_maskf_bf"] = (i[:, None] <= i[None, :]).astype(np.float32).astype(bf)
    c["c_maskb_bf"] = (i[:, None] >= i[None, :]).astype(np.float32).astype(bf)
    f = np.zeros((128, 5, 128), np.float32)
    f[:, 0, :] = np.eye(128)
    f[:, 1, :] = 1.0
    f[:, 2, :] = (i[:, None] <= i[None, :])
    f[:, 3, :] = (i[:, None] >= i[None, :])
    f[:, 4, :] = (i[:, None] % 32 == 0)
    c["c_f32"] = f
    t = np.arange(SEQ)
    row = (t // 64).astype(np.float64)
    col = (t % 64).astype(np.float64)
    nf = 32
    inv = 1.0 / (10000.0 ** (np.arange(nf, dtype=np.float64) / nf))
    ang_r = row[:, None] * inv
    ang_c = col[:, None] * inv
    ang = np.concatenate([ang_r, ang_r, ang_c, ang_c], axis=-1)
    cos = np.cos(ang).astype(np.float32)
    sin = np.sin(ang).astype(np.float32)
    sgn = np.concatenate([-np.ones(32), np.ones(32), -np.ones(32), np.ones(32)]).astype(np.float32)
    tab = np.stack([cos, sin * sgn[None, :]], axis=1)
    c["c_rope"] = np.ascontiguousarray(tab.reshape(NT, 128, 2, 128))
    return c


def build_program(nseq=SEQ_PER_CORE, dbg=None, stop_after=None):
    nc = bass.Bass("TRN2", target_bir_lowering=False)

    def din(name, shape, dt=F32):
        return nc.dram_tensor(name, shape, dt, kind="ExternalInput").ap()

    x = din("x", [nseq, SEQ, D])
    w_in = din("w_in", [D, DIN])
    w_out = din("w_out", [D, D])
    norm_g = din("norm_g", [1, D])
    b_gates = din("b_gates", [1, 16])
    q_norm_g = din("q_norm_g", [1, 128])
    k_norm_g = din("k_norm_g", [1, 128])
    mlstm_norm_g = din("mlstm_norm_g", [1, 1024])
    c_ident_bf = din("c_ident_bf", [128, 128], BF16)
    c_ones_bf = din("c_ones_bf", [128, 128], BF16)
    c_maskf_bf = din("c_maskf_bf", [128, 128], BF16)
    c_maskb_bf = din("c_maskb_bf", [128, 128], BF16)
    c_f32 = din("c_f32", [128, 5, 128])
    c_rope = din("c_rope", [NT, 128, 2, 128])
    y = nc.dram_tensor("y", [nseq, SEQ, D], F32, kind="ExternalOutput").ap()
    dbg_out = {}
    if dbg:
        for name, shape in dbg.items():
            dbg_out[name] = nc.dram_tensor("dbg_" + name, shape, F32, kind="ExternalOutput").ap()

    w_in_v = w_in.rearrange("(kc p) c -> p kc c", p=128)
    w_out_v = w_out.rearrange("(kc p) c -> p kc c", p=128)

    with ExitStack() as es:
        S = Sched(nc, es)
        V, A, P = nc.vector, nc.scalar, nc.tensor

        hT = S.sb("hT", [128, KC, SEQ], BF16)
        mixT = S.sb("mixT", [128, KC, SEQ], BF16)
        WR = [S.sb("WR%d" % i, [128, KC, 256], BF16) for i in range(3)]
        arena = S.sb("arena", [128, ARENA_BYTES // 2], BF16)
        identb = S.sb("identb", [128, 128], BF16)
        onesb = S.sb("onesb", [128, 128], BF16)
        maskb_t = [S.sb("maskf", [128, 128], BF16), S.sb("maskb", [128, 128], BF16)]
        cf = S.sb("cf", [128, 5, 128], F32)
        gcol = S.sb("gcol", [128, KC], F32)
        gqk = S.sb("gqk", [128, 3, 128], F32)
        biasb = S.sb("biasb", [128, 16], F32)
        Wg = S.sb("Wg", [128, KC, 16], BF16)
        Graw = S.sb("Graw", [128, NT, 16], F32)
        gt = {n: S.sb("g_" + n, [128, NT, 8], F32) for n in
              ["gi", "lf", "bb", "gbc", "imx", "Mx", "lam", "eX", "thr", "tA"]}
        mE = S.sb("g_mE", [128, NT + 1, 8], F32)
        st = S.sb("stats", [128, 96], F32)
        gmcol = S.sb("gmcol", [128, 8], F32)

        PP = [S.ps("PP%d" % i, [128, 1024], F32) for i in range(4)]

        def carve(off, shape, dt):
            n = int(np.prod(shape[1:]))
            nb = n * (4 if dt == F32 else 2)
            assert off % 4 == 0 and off + nb <= ARENA_BYTES, (off, nb)
            v = arena[:, off // 2:(off + nb) // 2]
            if dt == F32:
                v = v.bitcast(F32)
            if len(shape) == 3:
                v = v.rearrange("p (a b) -> p a b", a=shape[1])
            elif len(shape) == 4:
                v = v.rearrange("p (a b c) -> p a b c", a=shape[1], b=shape[2])
            return v

        S.dmas('sp', [
            (identb[:], c_ident_bf[:]), (onesb[:], c_ones_bf[:]),
            (maskb_t[0][:], c_maskf_bf[:]), (maskb_t[1][:], c_maskb_bf[:]),
            (cf[:], c_f32[:]),
            (gcol[:], norm_g.rearrange("o (kc p) -> p (o kc)", p=128)),
            (gqk[:, 0, :], q_norm_g.partition_broadcast(128)),
            (gqk[:, 1, :], q_norm_g.partition_broadcast(128)),
            (gqk[:, 2, :], k_norm_g.partition_broadcast(128)),
            (biasb[:], b_gates.partition_broadcast(128)),
            (gmcol[:], mlstm_norm_g.rearrange("o (j p) -> p (o j)", p=128)),
        ], writes=['const'], sem='cst', allow_slow_non_contiguous=True)
        S.dma('pool', Wg[:], w_in_v[:, :, C_G:C_G + 16], writes=['Wg'], sem='cst2')
        ident_f = cf[:, 0, :]
        ones_f = cf[:, 1, :]
        tri_f = cf[:, 2, :]
        tri_b = cf[:, 3, :]
        sel_f = cf[:, 4, :]
        diag = carve(42240, [128, 128], F32)

        plan = []
        for s_ in range(nseq):
            for g in range(2):
                for a in range(2):
                    plan.append(("A", [('in', C_AQ + (4 * g + 2 * a) * 128, 256)]))
                    if a == 0:
                        plan.append(("B", [('in', C_AK + g * 128, 128), ('in', C_AV + g * 128, 128)]))
                    plan.append(("Z", [('in', C_AZ + (4 * g + 2 * a) * 128, 256)]))
            for h in range(4):
                plan.append(("QK", [('in', C_MQ + h * 128, 128), ('in', C_MK + h * 128, 128)]))
                plan.append(("V", [('in', C_MV + h * 256, 256)]))
                plan.append(("O", [('in', C_MO + h * 256, 256)]))
                plan.append(("MZ", [('in', C_MZ + h * 256, 256)]))
            for nb in range(4):
                plan.append(("OA", [('out', nb * 512, 256)]))
                plan.append(("OB", [('out', nb * 512 + 256, 256)]))
        wstate = {"issued": 0, "cur": 0, "rel": set()}

        def w_issue():
            while wstate["issued"] < len(plan):
                k = wstate["issued"]
                if k >= 3 and (k - 3) not in wstate["rel"]:
                    break
                if k > wstate["cur"] + 2:
                    break
                slot = k % 3
                pairs = []
                off = 0
                for (src, c0, ncol) in plan[k][1]:
                    sv = w_in_v if src == 'in' else w_out_v
                    pairs.append((WR[slot][:, :, off:off + ncol], sv[:, :, c0:c0 + ncol]))
                    off += ncol
                S.dmas('pool', pairs, writes=['WR%d' % slot], sem='wr%d' % slot)
                wstate["issued"] += 1

        def w_get(tag):
            k = wstate["cur"]
            assert plan[k][0] == tag, (plan[k][0], tag, k)
            w_issue()
            assert wstate["issued"] > k, ("weight ring deadlock", k, tag)
            wstate["cur"] += 1
            return WR[k % 3], 'WR%d' % (k % 3), k

        def w_rel(k):
            wstate["rel"].add(k)
            w_issue()

        def hres(t0, t1):
            return ['hT%d' % t for t in range(t0, t1)]

        pj = [PP[0][:, 0:512], PP[0][:, 512:1024]]
        PPb1 = PP[1].bitcast(BF16)
        ptr3 = PPb1[:, 0:384].rearrange("p (h e) -> p h e", h=3)
        ptr2 = [PPb1[:, 0:128], PPb1[:, 128:256]]
        pgt = PP[1][:, 512:528]
        pz = PP[1][:, 512:1024]
        pzq = [PP[1][:, 512 + 128 * i:640 + 128 * i] for i in range(4)]
        PZQ = ['B3']
        pgq = [PP[1][:, 128 * i:128 * (i + 1)] for i in range(4)]
        ps_ = [PP[2][:, 0:512], PP[2][:, 512:1024]]
        ptrm = [PP[0].bitcast(BF16)[:, 0:128], PP[0].bitcast(BF16)[:, 1024:1152]]
        psm = [PP[1][:, 0:128], PP[1][:, 512:640]]
        po = PP[3][:, 0:512]
        pr = PP[3][:, 512:1024]


        def gates_pipeline():
            G5 = Graw[:, :, :].rearrange("p t (d k h) -> p t d k h", d=2, k=2)
            B4 = biasb[:, :].rearrange("p (d k h) -> p d k h", d=2, k=2)
            gi, lf, bb, gbc, imx, Mx, lam, eX, thr, tA = (gt[n] for n in
                                                         ["gi", "lf", "bb", "gbc", "imx", "Mx", "lam", "eX", "thr", "tA"])

            def v4(t):
                return t[:, :, :].rearrange("p t (d h) -> p t d h", d=2)

            def f2(t):
                return t[:, :, :].rearrange("p t c -> p (t c)")

            S.op('dve', lambda: V.tensor_tensor(v4(gi), G5[:, :, :, 0, :],
                                                 B4[:, None, :, 0, :].to_broadcast([128, NT, 2, 4]), ALU.add),
                 reads=['Graw', 'const'], writes=['gi'])
            S.op('dve', lambda: V.tensor_tensor(v4(lf), G5[:, :, :, 1, :],
                                                 B4[:, None, :, 1, :].to_broadcast([128, NT, 2, 4]), ALU.add),
                 reads=['Graw', 'const'], writes=['lf'])
            S.op('act', lambda: A.activation(f2(lf), f2(lf), AF.Exp, scale=-1.0), reads=['lf'], writes=['lf'])
            S.op('act', lambda: A.activation(f2(lf), f2(lf), AF.Ln, bias=1.0), reads=['lf'], writes=['lf'])
            S.op('dve', lambda: V.tensor_scalar(f2(lf), f2(lf), -1.0, None, ALU.mult), reads=['lf'], writes=['lf'])
            S.op('pe', lambda: P.matmul(pgq[0], tri_f, f2(lf), start=True, stop=True), reads=['lf', 'const'], writes=['B2'])
            S.op('pe', lambda: P.matmul(pgq[1], tri_b, f2(lf), start=True, stop=True), reads=['lf', 'const'], writes=['B2'])
            S.op('pe', lambda: P.matmul(pgq[2], ones_f, f2(lf), start=True, stop=True), reads=['lf', 'const'], writes=['B2'])
            p0 = pgq[0].rearrange("p (t c) -> p t c", c=8)
            p1 = pgq[1].rearrange("p (t c) -> p t c", c=8)
            S.op('dve', lambda: V.tensor_copy(bb[:, :, 0:4], p0[:, :, 0:4]), reads=['B2'], writes=['bb'])
            S.op('dve', lambda: V.tensor_copy(bb[:, :, 4:8], p1[:, :, 4:8]), reads=['B2'], writes=['bb'])
            S.op('act', lambda: A.copy(f2(gbc), pgq[2]), reads=['B2'], writes=['gbc'])
            S.op('pe', lambda: P.matmul(pgq[3], f2(gi), ident_f, start=True, stop=True), reads=['gi', 'const'], writes=['B2'])
            S.op('dve', lambda: V.tensor_reduce(st[:, 56:57], pgq[3], AX.X, ALU.max), reads=['B2'], writes=['imcol'])
            S.op('dve', lambda: V.tensor_scalar(diag[:], ident_f, st[:, 56:57], None, ALU.mult),
                 reads=['imcol', 'const'], writes=['diag'])
            S.op('pe', lambda: P.matmul(pgq[0], ones_f, diag[:], start=True, stop=True), reads=['diag', 'const'], writes=['B2'])
            S.op('dve', lambda: V.tensor_copy(f2(imx), pgq[0]), reads=['B2'], writes=['imx'])
            S.op('dve', lambda: V.memset(mE[:], M0), writes=['mE'])
            for dh in range(4):
                S.op('dve', lambda dh=dh: V.tensor_tensor_scan(mE[:, 1:NT + 1, dh], gbc[:, :, dh], imx[:, :, dh], M0,
                                                                ALU.add, ALU.max),
                     reads=['gbc', 'imx', 'mE'], writes=['mE'])
            for dh in range(4, 8):
                S.op('dve', lambda dh=dh: V.tensor_tensor_scan(mE[:, 0:NT, dh][:, ::-1], gbc[:, :, dh][:, ::-1],
                                                                imx[:, :, dh][:, ::-1], M0, ALU.add, ALU.max),
                     reads=['gbc', 'imx', 'mE'], writes=['mE'])
            S.op('dve', lambda: V.tensor_tensor(Mx[:, :, 0:4], mE[:, 0:NT, 0:4], imx[:, :, 0:4], ALU.max),
                 reads=['mE', 'imx'], writes=['Mx'])
            S.op('dve', lambda: V.tensor_tensor(Mx[:, :, 4:8], mE[:, 1:NT + 1, 4:8], imx[:, :, 4:8], ALU.max),
                 reads=['mE', 'imx'], writes=['Mx'])
            S.op('dve', lambda: V.tensor_tensor(tA[:, :, :], gbc[:, :, :], Mx[:, :, :], ALU.add), reads=['gbc', 'Mx'], writes=['tA'])
            S.op('dve', lambda: V.tensor_tensor(tA[:, 0:NT - 1, 0:4], tA[:, 0:NT - 1, 0:4], Mx[:, 1:NT, 0:4], ALU.subtract),
                 reads=['tA', 'Mx'], writes=['tA'])
            S.op('dve', lambda: V.tensor_tensor(tA[:, 1:NT, 4:8], tA[:, 1:NT, 4:8], Mx[:, 0:NT - 1, 4:8], ALU.subtract),
                 reads=['tA', 'Mx'], writes=['tA'])
            S.op('act', lambda: A.activation(f2(lam), f2(tA), AF.Exp), reads=['tA'], writes=['lam'])
            S.op('dve', lambda: V.tensor_tensor(tA[:, :, :], gi[:, :, :], bb[:, :, :], ALU.subtract), reads=['gi', 'bb'], writes=['tA'])
            S.op('dve', lambda: V.tensor_tensor(tA[:, :, :], tA[:, :, :], Mx[:, :, :], ALU.subtract), reads=['tA', 'Mx'], writes=['tA'])
            S.op('act', lambda: A.activation(f2(eX), f2(tA), AF.Exp), reads=['tA'], writes=['eX'])
            S.op('dve', lambda: V.tensor_tensor(tA[:, :, :], bb[:, :, :], Mx[:, :, :], ALU.add), reads=['bb', 'Mx'], writes=['tA'])
            S.op('act', lambda: A.activation(f2(thr), f2(tA), AF.Exp, scale=-1.0), reads=['tA'], writes=['thr'])

        def mlstm_phase(s_):
            qTm = carve(0, [128, SEQ], BF16)
            kTm = carve(4096, [128, SEQ], BF16)
            vext = carve(8192, [128, NT, 257], BF16)
            hsum = carve(16416, [128, NT, 256], F32)
            U = [carve(32800, [128, 258], F32), carve(33832, [128, 258], F32)]
            Cs = [carve(34864, [128, 258], BF16), carve(35380, [128, 258], BF16)]
            Pm = [[carve(35896, [128, 128], BF16), carve(42048, [128, 128], BF16), carve(43072, [128, 128], BF16)],
                  [carve(36152, [128, 128], BF16), carve(42304, [128, 128], BF16), carve(43328, [128, 128], BF16)]]
            kx = [[carve(36408, [128, 128], BF16), carve(42560, [128, 128], BF16), carve(43584, [128, 128], BF16)],
                  [carve(36664, [128, 128], BF16), carve(42816, [128, 128], BF16), carve(43840, [128, 128], BF16)]]
            sozb = [carve(36920, [128, 512], F32), carve(38968, [128, 512], F32), carve(32800, [128, 512], F32)]
            obb = [carve(41016, [128, 256], BF16), carve(41528, [128, 256], BF16), carve(34864, [128, 256], BF16)]
            htmp = [carve(36920, [128, 256], F32), carve(37944, [128, 256], F32)]
            ALIAS = ['U0', 'U1', 'Cs0', 'soz2', 'soz2z', 'soz2o', 'ob2', 'soz0', 'soz0z', 'soz0o', 'htmp0', 'htmp1']
            pjf = [PP[0][:, 0:512], PP[0][:, 512:1024], PP[2][:, 0:512]]
            pjfn = ['B0', 'B1', 'B4']
            eX, thr, lam = gt['eX'], gt['thr'], gt['lam']
            S.op('dve', lambda: V.memset(vext[:, :, 256:257], 1.0), writes=['vones'])
            for h in range(4):
                WQK, WQKr, kQK = w_get("QK")
                WV, WVr, kV = w_get("V")
                for tb in range(4):
                    ts = slice(tb * 512, (tb + 1) * 512)
                    S.group('pe', [(lambda kc=kc: P.matmul(pj[0], WQK[:, kc, 0:128], hT[:, kc, ts],
                                                           start=(kc == 0), stop=(kc == KC - 1))) for kc in range(KC)],
                            reads=hres(4 * tb, 4 * tb + 4) + [WQKr], writes=['B0'])
                    S.op('act', lambda: A.activation(qTm[:, ts], pj[0], AF.Copy, scale=128.0 ** -0.5),
                         reads=['B0'], writes=['qTm'])
                    S.group('pe', [(lambda kc=kc: P.matmul(pj[1], WQK[:, kc, 128:256], hT[:, kc, ts],
                                                           start=(kc == 0), stop=(kc == KC - 1))) for kc in range(KC)],
                            reads=hres(4 * tb, 4 * tb + 4) + [WQKr], writes=['B1'])
                    S.op('dve', lambda: V.tensor_copy(kTm[:, ts], pj[1]), reads=['B1'], writes=['kTm'])
                for T in range(NT):
                    b = T % 2
                    S.group('pe', [(lambda kc=kc: P.matmul(pj[b][:, 0:256], hT[:, kc, T * 128:(T + 1) * 128], WV[:, kc, :],
                                                           start=(kc == 0), stop=(kc == KC - 1))) for kc in range(KC)],
                            reads=['hT%d' % T, WVr], writes=['B%d' % b])
                    if b == 0:
                        S.op('act', lambda: A.copy(vext[:, T, 0:256], pj[b][:, 0:256]), reads=['B%d' % b], writes=['vext'])
                    else:
                        S.op('dve', lambda: V.tensor_copy(vext[:, T, 0:256], pj[b][:, 0:256]), reads=['B%d' % b], writes=['vext'])
                w_rel(kQK)
                w_rel(kV)
                def ch(i, dr):
                    c = i if dr == 0 else NT - 1 - i
                    return c, (c - 1 if dr == 0 else c + 1), dr * 4 + h, slice(c * 128, (c + 1) * 128)

                def emit_front(i):
                    for dr in range(2):
                        c, cprev, dh, cs = ch(i, dr)
                        eXc = eX[:, c, dh:dh + 1]
                        S.op('pe', lambda: P.transpose(ptrm[dr], kTm[:, cs], identb[:]),
                             reads=['kTm', 'const'], writes=['B%d' % dr])
                        S.op('pe', lambda: P.matmul(psm[dr], kTm[:, cs], qTm[:, cs], start=True, stop=True),
                             reads=['kTm', 'qTm'], writes=['B%d' % (2 + dr)])
                        S.op('act', lambda: A.activation(kx[dr][i % 3], ptrm[dr], AF.Copy, scale=eXc),
                             reads=['B%d' % dr, 'eX'], writes=['kx%d%d' % (dr, i % 3)])
                        S.op('dve', lambda: V.scalar_tensor_tensor(Pm[dr][i % 3], psm[dr], eXc, maskb_t[dr][:], ALU.mult, ALU.mult),
                             reads=['B%d' % (2 + dr), 'eX', 'const'], writes=['Pm%d%d' % (dr, i % 3)])

                def emit_dC(i):
                    for dr in range(2):
                        c, cprev, dh, cs = ch(i, dr)
                        pCn = 'B6' if dr == 0 else 'B7'
                        pdC = (po if dr == 0 else pr)[:, 0:257]
                        S.op('pe', lambda: P.matmul(pdC, kx[dr][i % 3], vext[:, c, :], start=True, stop=True),
                             reads=['kx%d%d' % (dr, i % 3), 'vext', 'vones'], writes=[pCn])
                    for dr in range(2):
                        c, cprev, dh, cs = ch(i, dr)
                        pCn = 'B6' if dr == 0 else 'B7'
                        pdC = (po if dr == 0 else pr)[:, 0:257]
                        if i == 0:
                            S.op('dve', lambda: V.tensor_copy(U[dr][:, 0:257], pdC), reads=[pCn], writes=['U%d' % dr])
                        else:
                            lamp = lam[:, cprev, dh:dh + 1]
                            S.op('dve', lambda: V.scalar_tensor_tensor(U[dr][:, 0:257], U[dr][:, 0:257], lamp, pdC,
                                                                        ALU.mult, ALU.add),
                                 reads=['U%d' % dr, pCn, 'lam'], writes=['U%d' % dr])

                def emit_cs(i):
                    for dr in range(2):
                        c, cprev, dh, cs = ch(i, dr)
                        if i < NT - 1:
                            lamc = lam[:, c, dh:dh + 1]
                            if i >= NT // 2 and dr == 0:
                                S.op('act', lambda: A.activation(Cs[dr][:, 0:257], U[dr][:, 0:257], AF.Copy, scale=lamc),
                                     reads=['U%d' % dr, 'lam'], writes=['Cs%d' % dr])
                            else:
                                S.op('pool', lambda: nc.gpsimd.tensor_tensor(Cs[dr][:, 0:257], U[dr][:, 0:257], lamc.to_broadcast([128, 257]), ALU.mult),
                                     reads=['U%d' % dr, 'lam'], writes=['Cs%d' % dr])

                def emit_N(i):
                    for dr in range(2):
                        c, cprev, dh, cs = ch(i, dr)
                        pNn = 'B%d' % (4 + dr)
                        pN = ps_[dr][:, 0:257]
                        fns = [lambda: P.matmul(pN, Pm[dr][i % 3], vext[:, c, :], start=True, stop=(i == 0))]
                        if i > 0:
                            fns.append(lambda: P.matmul(pN, qTm[:, cs], Cs[dr][:, 0:257], start=False, stop=True))
                        S.group('pe', fns, reads=['Pm%d%d' % (dr, i % 3), 'vext', 'vones', 'qTm', 'Cs%d' % dr], writes=[pNn])
                    emit_cs(i)
                    for dr in range(2):
                        c, cprev, dh, cs = ch(i, dr)
                        thrc = thr[:, c, dh:dh + 1]
                        pNn = 'B%d' % (4 + dr)
                        pN = ps_[dr][:, 0:257]
                        dd = st[:, 58 + dr:59 + dr]
                        rec = st[:, 60 + dr:61 + dr]
                        S.op('dve', lambda: V.tensor_scalar(dd, pN[:, 256:257], thrc, None, ALU.max),
                             reads=[pNn, 'thr'], writes=['hd%d' % dr])
                        S.op('dve', lambda: V.scalar_tensor_tensor(dd, pN[:, 256:257], -1.0, dd, ALU.mult, ALU.max),
                             reads=[pNn, 'hd%d' % dr], writes=['hd%d' % dr])
                        S.op('dve', lambda: V.reciprocal(rec, dd), reads=['hd%d' % dr], writes=['hr%d' % dr])
                        if i < NT // 2:
                            S.op('act', lambda: A.activation(hsum[:, c, :], pN[:, 0:256], AF.Copy, scale=rec),
                                 reads=[pNn, 'hr%d' % dr], writes=['hsum%d' % c])
                        else:
                            S.op('act', lambda: A.activation(htmp[dr], pN[:, 0:256], AF.Copy, scale=rec),
                                 reads=[pNn, 'hr%d' % dr], writes=['htmp%d' % dr])
                            S.op('pool', lambda: nc.gpsimd.tensor_tensor(hsum[:, c, :], hsum[:, c, :], htmp[dr], ALU.add),
                                 reads=['htmp%d' % dr, 'hsum%d' % c], writes=['hsum%d' % c])

                S.fence('dve', lambda: V.memset(st[:, 95:96], 0.0), ALIAS)
                emit_front(0)
                emit_front(1)
                for i in range(NT):
                    emit_dC(i)
                    if i + 2 < NT:
                        emit_front(i + 2)
                    emit_N(i)
                S.fence('dve', lambda: V.memset(st[:, 95:96], 0.0), ALIAS)
                WO, WOr, kO = w_get("O")
                WZ, WZr, kMZ = w_get("MZ")

                def fin_proj(T):
                    b = T % 3
                    Ts = slice(T * 128, (T + 1) * 128)
                    fns = [(lambda kc=kc: P.matmul(pjf[b][:, 0:256], hT[:, kc, Ts], WO[:, kc, :],
                                                   start=(kc == 0), stop=(kc == KC - 1))) for kc in range(KC)]
                    fns += [(lambda kc=kc: P.matmul(pjf[b][:, 256:512], hT[:, kc, Ts], WZ[:, kc, :],
                                                    start=(kc == 0), stop=(kc == KC - 1))) for kc in range(KC)]
                    S.group('pe', fns, reads=['hT%d' % T, WOr, WZr], writes=[pjfn[b]])

                def fin_one(T):
                    b = T % 3
                    soz = sozb[b]
                    sn, on = 'soz%d' % b, 'ob%d' % b
                    c_ss, c_rl, c_rs = 64 + b, 68 + b, 72 + b
                    S.op('act', lambda: A.activation(soz, pjf[b], AF.Exp, scale=-1.0), reads=[pjfn[b]],
                         writes=[sn, sn + 'z', sn + 'o'])
                    S.op('act', lambda: A.activation(soz, soz, AF.Ln, bias=1.0), reads=[sn], writes=[sn])
                    S.op('act', lambda: A.activation(soz, soz, AF.Exp, scale=-1.0), reads=[sn], writes=[sn])
                    S.op('dve', lambda: V.tensor_tensor(soz[:, 256:512], soz[:, 256:512], pjf[b][:, 256:512], ALU.mult),
                         reads=[sn, pjfn[b]], writes=[sn + 'z'])
                    S.op('pool', lambda: nc.gpsimd.tensor_tensor(soz[:, 0:256], soz[:, 0:256], hsum[:, T, :], ALU.mult),
                         reads=[sn, 'hsum%d' % T], writes=[sn + 'o'])
                    S.op('act', lambda: A.activation(obb[b], soz[:, 0:256], AF.Square, accum_out=st[:, c_ss:c_ss + 1]),
                         reads=[sn + 'o'], writes=[on, 'ssm%d' % b])
                    S.op('act', lambda: A.activation(st[:, c_rl:c_rl + 1], st[:, c_ss:c_ss + 1], AF.Ln, bias=EPS, scale=1.0 / 256),
                         reads=['ssm%d' % b], writes=['rlm%d' % b])
                    S.op('act', lambda: A.activation(st[:, c_rs:c_rs + 1], st[:, c_rl:c_rl + 1], AF.Exp, scale=-0.5),
                         reads=['rlm%d' % b], writes=['rsm%d' % b])

                def fin_two(T):
                    b = T % 3
                    Ts = slice(T * 128, (T + 1) * 128)
                    soz, ob = sozb[b], obb[b]
                    sn, on = 'soz%d' % b, 'ob%d' % b
                    c_rs = 72 + b
                    S.op('dve', lambda: V.scalar_tensor_tensor(ob, soz[:, 0:256], st[:, c_rs:c_rs + 1], soz[:, 256:512],
                                                                ALU.mult, ALU.mult),
                         reads=[sn + 'o', sn + 'z', 'rsm%d' % b], writes=[on])
                    S.group('pe', [(lambda k=k: P.transpose(ptr2[k], ob[:, k * 128:(k + 1) * 128], identb[:])) for k in range(2)],
                            reads=[on, 'const'], writes=['B2'])
                    for k in range(2):
                        S.op('dve', lambda: V.tensor_scalar(mixT[:, 8 + 2 * h + k, Ts], ptr2[k],
                                                             gmcol[:, 2 * h + k:2 * h + k + 1], None, ALU.mult),
                             reads=['B2', 'const'], writes=['mix%d' % T])

                fin_proj(0)
                fin_proj(1)
                fin_one(0)
                for T in range(NT):
                    if T + 2 < NT:
                        fin_proj(T + 2)
                    if T + 1 < NT:
                        fin_one(T + 1)
                    fin_two(T)
                w_rel(kO)
                w_rel(kMZ)

        xinA = [carve(8192, [128, D], F32), carve(16384, [128, D], F32), carve(32768, [128, D], F32)]
        xnA = [carve(24576, [128, D], BF16), carve(28672, [128, D], BF16)]
        PA = [PP[2].bitcast(BF16).rearrange("p (k e) -> p k e", k=KC),
              PP[3].bitcast(BF16).rearrange("p (k e) -> p k e", k=KC)]
        PAR = [['B4', 'B5'], ['B6', 'B7']]

        def phaseA_load(sq_, T, bufs=None):
            b = T % 2
            xin_l, xn = (xinA, xnA) if bufs is None else (bufs[0], bufs[1])
            bx = T % len(xin_l)
            xin = {b: xin_l[bx]}
            S.dma('sp', xin[b], x[sq_, T * 128:(T + 1) * 128, :], writes=['xin%d' % bx], sem='xin%d' % bx)
            S.op('act', lambda: A.activation(xn[b], xin[b], AF.Square, accum_out=st[:, b:b + 1]),
                 reads=['xin%d' % bx], writes=['xn%d' % b, 'ss%d' % b])
            S.op('act', lambda: A.activation(st[:, 2 + b:3 + b], st[:, b:b + 1], AF.Ln, bias=EPS, scale=1.0 / D),
                 reads=['ss%d' % b], writes=['rl%d' % b])
            S.op('act', lambda: A.activation(st[:, 4 + b:5 + b], st[:, 2 + b:3 + b], AF.Exp, scale=-0.5),
                 reads=['rl%d' % b], writes=['rs%d' % b])
            S.op('act', lambda: A.activation(xn[b], xin[b], AF.Copy, scale=st[:, 4 + b:5 + b]),
                 reads=['xin%d' % bx, 'rs%d' % b], writes=['xn%d' % b])

        def phaseA_xpose(sq_, T, bufs=None):
            b = T % 2
            xn = xnA if bufs is None else bufs[1]
            pa, par = (PA[b], PAR[b]) if bufs is None else (PA[1], PAR[1])
            S.group('pe', [(lambda kc=kc: P.transpose(pa[:, kc, :], xn[b][:, kc * 128:(kc + 1) * 128], identb[:]))
                           for kc in range(KC)],
                    reads=['xn%d' % b, 'const'], writes=par)
            S.op('dve', lambda: V.tensor_tensor(hT[:, :, T * 128:(T + 1) * 128], pa,
                                                 gcol[:, :].unsqueeze(2).to_broadcast([128, KC, 128]), ALU.mult),
                 reads=par + ['const'], writes=['hT%d' % T])

        def phaseA_tile(sq_, T):
            phaseA_load(sq_, T)
            phaseA_xpose(sq_, T)


        N_A = ['xin0', 'xin1', 'xin2', 'xn0', 'xn1']
        N_C = ['xres%d' % i for i in range(5)] + ['ysb%d' % i for i in range(3)]
        N_ATT = ([p + str(r) for r in range(4) for p in ('xg', 't1', 't2a', 't2b', 'qr', 'rope')] + ['sqj', 'diag', 'prs0', 'prs1', 'selb', 'rhl0', 'rhl1']
                 + ['PT%d' % r for r in range(2, 8)])
        N_HI = ['qT', 'kT', 'vsb', 'PT0', 'PT1', 'szb0', 'szb1', 'rinv0', 'rinv1', 'ot0', 'ot1']
        N_MIX = ['mix%d' % t for t in range(NT)]
        N_ML = (['qTm', 'kTm', 'vext', 'vones'] + ['hsum%d' % t for t in range(NT)]
                + ['U0', 'U1', 'Cs0', 'Cs1', 'htmp0', 'htmp1']
                + ['Pm%d%d' % (a_, b_) for a_ in range(2) for b_ in range(3)]
                + ['kx%d%d' % (a_, b_) for a_ in range(2) for b_ in range(3)]
                + ['soz%d%s' % (i, sfx_) for i in range(3) for sfx_ in ('', 'z', 'o')] + ['ob%d' % i for i in range(3)])

        def phase_fence(names=None):
            if names is None:
                names = N_A + N_C + N_ATT + N_HI + N_MIX + N_ML
            S.fence('dve', lambda: V.memset(st[:, 94:95], 0.0), names)

        mx_lo = mixT[:, 0:8, :].rearrange("p k t -> p (k t)")
        A0 = ([mx_lo[:, 0:4096].bitcast(F32), mx_lo[:, 4096:8192].bitcast(F32), mx_lo[:, 8192:12288].bitcast(F32)],
              [mx_lo[:, 12288:14336], mx_lo[:, 14336:16384]])

        for s_ in range(nseq):
            if stop_after == 'A':
                break
            mx_hi = mixT[:, 8:16, :].rearrange("p k t -> p (k t)")

            def carve2(off, shape, dt):
                n = int(np.prod(shape[1:]))
                nb = n * (4 if dt == F32 else 2)
                assert off % 4 == 0 and off + nb <= 32768
                v = mx_hi[:, off // 2:(off + nb) // 2]
                if dt == F32:
                    v = v.bitcast(F32)
                if len(shape) == 3:
                    v = v.rearrange("p (a b) -> p a b", a=shape[1])
                return v
            qT = carve2(0, [128, 2, SEQ], BF16)
            kT = carve2(8192, [128, SEQ], BF16)
            vsb = carve2(12288, [128, NT, 128], BF16)
            PT = [carve2(16384, [128, 1024], BF16), carve2(30720, [128, 1024], BF16)]
            szb = [carve2(18432, [128, 512], F32), carve2(20480, [128, 512], F32)]
            rinv = [carve2(22528, [128, 512], F32), carve2(24576, [128, 512], F32)]
            ot = [carve2(26624, [128, 512], F32), carve2(28672, [128, 512], F32)]
            RA = 4
            xg = [carve(5376 * r, [128, 3, 128], F32) for r in range(RA)]
            t1 = [carve(5376 * r + 1536, [128, 3, 128], F32) for r in range(RA)]
            t2 = [carve(5376 * r + 3072, [128, 3, 128], F32) for r in range(RA)]
            qr = [carve(5376 * r + 4608, [128, 3, 128], BF16) for r in range(RA)]
            sqj = carve(5376 * RA, [128, 128], BF16)
            rope = [carve(5376 * RA + 256 + 1024 * r, [128, 2, 128], F32) for r in range(RA)]
            PT = PT + [carve(25856 + 2048 * r, [128, 1024], BF16) for r in range(6)]
            prs = [carve(38144, [128, 512], F32)] * 2
            selb = carve(42752, [128, 128], BF16)
            rhl = [carve(40192, [128, 1024], BF16)] * 2
            S.op('act', lambda: A.copy(selb, sel_f), reads=['const'], writes=['selb'])
            pjr = [PP[0][:, 0:512], PP[0][:, 512:1024], PP[2][:, 0:512], PP[2][:, 512:1024]]
            pjrn = ['B0', 'B1', 'B4', 'B5']
            nblk = 0
            bg = []
            for g in range(2):
                for a in range(2):
                    nh = 3 if a == 0 else 2
                    WA, WAr, kA = w_get("A")
                    if a == 0:
                        WB, WBr, kB = w_get("B")
                    gate_pass = (g == 0 and a == 0)

                    def st_proj(T):
                        pjb = T % RA
                        pjn = pjrn[pjb]
                        S.dma('sp', rope[pjb], c_rope[T], writes=['rope%d' % pjb], sem='rope%d' % pjb)
                        fns = [(lambda kc=kc: P.matmul(pjr[pjb][:, 0:256], hT[:, kc, T * 128:(T + 1) * 128], WA[:, kc, :],
                                                       start=(kc == 0), stop=(kc == KC - 1))) for kc in range(KC)]
                        rd = ['hT%d' % T, WAr]
                        if a == 0:
                            fns += [(lambda kc=kc: P.matmul(pjr[pjb][:, 256:512], hT[:, kc, T * 128:(T + 1) * 128], WB[:, kc, :],
                                                            start=(kc == 0), stop=(kc == KC - 1))) for kc in range(KC)]
                            rd.append(WBr)
                        S.group('pe', fns, reads=rd, writes=[pjn])
                        if gate_pass:
                            pgr = PP[1][:, 512 + 16 * pjb:528 + 16 * pjb]
                            S.group('pe', [(lambda kc=kc: P.matmul(pgr, hT[:, kc, T * 128:(T + 1) * 128], Wg[:, kc, :],
                                                                   start=(kc == 0), stop=(kc == KC - 1))) for kc in range(KC)],
                                    reads=['hT%d' % T, 'Wg'], writes=['B3'])

                    def st_one(T):
                        pb = T % RA
                        rb = pb
                        pjn = pjrn[pb]
                        sfx = '%d' % pb
                        pv = pjr[pb][:, 0:nh * 128].rearrange("p (h e) -> p h e", h=nh)
                        c_ss, c_rl, c_rs = 8 + 4 * pb, 24 + 4 * pb, 40 + 4 * pb
                        if gate_pass:
                            S.op('dve', lambda: V.tensor_copy(Graw[:, T, :], PP[1][:, 512 + 16 * pb:528 + 16 * pb]),
                                 reads=['B3'], writes=['Graw'])
                        for h in range(nh):
                            S.op('act', lambda: A.activation(sqj, pv[:, h, :], AF.Square, accum_out=st[:, c_ss + h:c_ss + h + 1]),
                                 reads=[pjn], writes=['sqj', 'ssq' + sfx + str(h)])
                        gsel = gqk[:, 0:3, :] if a == 0 else gqk[:, 0:2, :]
                        S.op('dve', lambda: V.tensor_tensor(xg[pb][:, 0:nh, :], pv, gsel, ALU.mult),
                             reads=[pjn, 'const'], writes=['xg' + sfx])
                        if a == 0:
                            S.op('act', lambda: A.copy(vsb[:, T, :], pjr[pb][:, 384:512]), reads=[pjn], writes=['vsb'])
                        S.op('act', lambda: A.activation(st[:, c_rl:c_rl + nh], st[:, c_ss:c_ss + nh], AF.Ln, bias=EPS, scale=1.0 / 128),
                             reads=['ssq' + sfx + str(h) for h in range(nh)], writes=['rlq' + sfx])
                        S.op('act', lambda: A.activation(st[:, c_rs:c_rs + nh], st[:, c_rl:c_rl + nh], AF.Exp, scale=-0.5),
                             reads=['rlq' + sfx], writes=['rsq' + sfx])
                        x5 = xg[pb][:, 0:nh, :].rearrange("p h (r f e) -> p h r f e", r=2, f=2)
                        t5 = t2[pb][:, 0:nh, :].rearrange("p h (r f e) -> p h r f e", r=2, f=2)
                        s4 = rope[rb][:, 1, :].rearrange("p (r f e) -> p r f e", r=2, f=2)
                        S.op('pool', lambda: nc.gpsimd.tensor_tensor(t5[:, :, :, 0, :], x5[:, :, :, 1, :],
                                                                      s4[:, None, :, 0, :].to_broadcast([128, nh, 2, 32]), ALU.mult),
                             reads=['xg' + sfx, 'rope%d' % rb], writes=['t2a' + sfx])
                        S.op('pool', lambda: nc.gpsimd.tensor_tensor(t5[:, :, :, 1, :], x5[:, :, :, 0, :],
                                                                      s4[:, None, :, 1, :].to_broadcast([128, nh, 2, 32]), ALU.mult),
                             reads=['xg' + sfx, 'rope%d' % rb], writes=['t2b' + sfx])
                        S.op('dve', lambda: V.tensor_tensor(t1[pb][:, 0:nh, :], xg[pb][:, 0:nh, :],
                                                             rope[rb][:, 0:1, :].to_broadcast([128, nh, 128]), ALU.mult),
                             reads=['xg' + sfx, 'rope%d' % rb], writes=['t1' + sfx])
                        S.op('dve', lambda: V.tensor_tensor(t1[pb][:, 0:nh, :], t1[pb][:, 0:nh, :], t2[pb][:, 0:nh, :], ALU.add),
                             reads=['t1' + sfx, 't2a' + sfx, 't2b' + sfx], writes=['t1' + sfx])
                        S.op('pool', lambda: nc.gpsimd.tensor_tensor(
                            qr[pb][:, 0:nh, :], t1[pb][:, 0:nh, :],
                            st[:, c_rs:c_rs + nh].unsqueeze(2).to_broadcast([128, nh, 128]), ALU.mult),
                             reads=['t1' + sfx, 'rsq' + sfx], writes=['qr' + sfx])

                    def st_two(T):
                        pb = T % RA
                        sfx = '%d' % pb
                        S.group('pe', [(lambda h=h: P.transpose(ptr3[:, h, :], qr[pb][:, h, :], identb[:])) for h in range(nh)],
                                reads=['qr' + sfx, 'const'], writes=['B2'])
                        S.op('dve', lambda: V.tensor_copy(qT[:, :, T * 128:(T + 1) * 128], ptr3[:, 0:2, :]),
                             reads=['B2'], writes=['qT'])
                        if a == 0:
                            S.op('dve', lambda: V.tensor_copy(kT[:, T * 128:(T + 1) * 128], ptr3[:, 2, :]), reads=['B2'], writes=['kT'])

                    LA = RA - 1
                    if s_ == 0 and gate_pass:
                        for t in range(NT + 5):
                            if t < NT:
                                phaseA_load(0, t, A0)
                            if 0 <= t - 1 < NT:
                                phaseA_xpose(0, t - 1, A0)
                            if 0 <= t - 2 < NT:
                                st_proj(t - 2)
                            if 0 <= t - 4 < NT:
                                st_one(t - 4)
                            if 0 <= t - 5 < NT:
                                st_two(t - 5)
                        phase_fence(N_A + N_MIX)
                    else:
                        for T in range(LA):
                            st_proj(T)
                        st_one(0)
                        for T in range(NT):
                            if T + LA < NT:
                                st_proj(T + LA)
                            if T + 1 < NT:
                                st_one(T + 1)
                            st_two(T)
                    w_rel(kA)
                    if a == 0:
                        w_rel(kB)
                    if gate_pass:
                        S.deferred = []
                        gates_pipeline()
                        bg = S.deferred
                        S.deferred = None
                    WZ, WZr, kZ = w_get("Z")
                    blocks = [(hh, qb) for hh in range(2) for qb in range(4)]
                    binfo = {}
                    if gate_pass:
                        pzb = [PP[1][:, 512:1024], PP[1][:, 512:1024]]
                        pzbn = ['B3', 'B3']
                    else:
                        pzb = [PP[1][:, 512:1024], PP[1][:, 0:512]]
                        pzbn = ['B3', 'B2']
                    spair = [PP[2], PP[0]]
                    spn = [['B4', 'B5'], ['B0', 'B1']]

                    def blk_setup(n):
                        nonlocal nblk
                        hh, qb = blocks[n]
                        bp = nblk % 2
                        nblk += 1
                        binfo[n] = dict(hh=hh, qb=qb, qs=slice(qb * 512, (qb + 1) * 512), bp=bp, head=4 * g + 2 * a + hh)

                    def blk_zmm(n, k0, k1):
                        bi = binfo[n]
                        hh, qb, qs = bi['hh'], bi['qb'], bi['qs']
                        pzc, pzn = pzb[bi['bp']], pzbn[bi['bp']]
                        S.group('pe', [(lambda kc=kc: P.matmul(pzc, WZ[:, kc, hh * 128:(hh + 1) * 128], hT[:, kc, qs],
                                                               start=(kc == 0), stop=(kc == KC - 1))) for kc in range(k0, k1)],
                                reads=hres(4 * qb, 4 * qb + 4) + [WZr], writes=[pzn])

                    def blk_sig(n, step):
                        bp = binfo[n]['bp']
                        szc, szn = szb[bp], 'szb%d' % bp
                        pzc, pzn = pzb[bp], pzbn[bp]
                        if step == 0:
                            S.op('act', lambda: A.activation(szc, pzc, AF.Exp, scale=-1.0), reads=[pzn], writes=[szn])
                        elif step == 1:
                            S.op('act', lambda: A.activation(szc, szc, AF.Ln, bias=1.0), reads=[szn], writes=[szn])
                        elif step == 2:
                            S.op('act', lambda: A.activation(szc, szc, AF.Exp, scale=-1.0), reads=[szn], writes=[szn])
                        else:
                            S.op('dve', lambda: V.tensor_tensor(szc, szc, pzc, ALU.mult), reads=[szn, pzn], writes=[szn])

                    def blk_epi1(n):
                        bp = binfo[n]['bp']
                        fns = []
                        for kt in range(NT):
                            gq = kt % 4
                            fns.append(lambda kt=kt, gq=gq: P.matmul(pr[32 * gq:32 * gq + 32, :], onesb[:, 0:32],
                                                                     PT[kt // 2][:, 512 * (kt % 2):512 * (kt % 2 + 1)],
                                                                     start=(kt < 4), stop=(kt >= NT - 4), tile_position=(0, 32 * gq)))
                        S.group('pe', fns, reads=['const'] + ['PT%d' % r for r in range(8)], writes=['B7'])
                        S.op('dve', lambda: V.tensor_copy(ot[bp], po), reads=['B6'], writes=['ot%d' % bp])
                        S.op('dve', lambda: V.tensor_copy(prs[bp], pr), reads=['B7'], writes=['prs0'])
                        S.op('dve', lambda: V.tensor_copy(rhl[bp][:, 0:512], prs[bp]), reads=['prs0'], writes=['rhl0'])
                        S.op('dve', lambda: V.tensor_tensor(rhl[bp][:, 512:1024], prs[bp], rhl[bp][:, 0:512], ALU.subtract),
                             reads=['prs0', 'rhl0'], writes=['rhl0'])

                    def blk_rs_mm(n):
                        bp = binfo[n]['bp']
                        S.op('pe', lambda: P.matmul(pr, selb, rhl[bp][:, 0:512], start=True, stop=False),
                             reads=['selb', 'rhl0'], writes=['B7'])
                        S.op('pe', lambda: P.matmul(pr, selb, rhl[bp][:, 512:1024], start=False, stop=True),
                             reads=['selb', 'rhl0'], writes=['B7'])

                    def blk_rs(n):
                        bp = binfo[n]['bp']
                        S.op('act', lambda: A.activation(rinv[bp], pr, AF.Ln), reads=['B7'], writes=['rinv%d' % bp])

                    def blk_epi2(n):
                        bi = binfo[n]
                        qb, qs, bp, head = bi['qb'], bi['qs'], bi['bp'], bi['head']
                        szc, ric, otc = szb[bp], rinv[bp], ot[bp]
                        szn, rin, otn = 'szb%d' % bp, 'rinv%d' % bp, 'ot%d' % bp
                        S.op('act', lambda: A.activation(ric, ric, AF.Exp, scale=-1.0), reads=[rin], writes=[rin])
                        S.op('dve', lambda: V.tensor_tensor(otc, otc, ric, ALU.mult), reads=[otn, rin], writes=[otn])
                        S.op('pool', lambda: nc.gpsimd.tensor_tensor(mixT[:, head, qs], otc, szc, ALU.mult),
                             reads=[otn, szn], writes=['mix%d' % t for t in range(4 * qb, 4 * qb + 4)])

                    NJ = NT // 2
                    blk_setup(0)
                    blk_zmm(0, 0, KC)
                    for n in range(len(blocks)):
                        bi = binfo[n]
                        hh, qs = bi['hh'], bi['qs']

                        def s_pair(j):
                            sb_ = spair[j % 2]
                            S.group('pe', [(lambda u=u: P.matmul(sb_[:, 512 * u:512 * (u + 1)],
                                                                 kT[:, (2 * j + u) * 128:(2 * j + u + 1) * 128], qT[:, hh, qs],
                                                                 start=True, stop=True)) for u in range(2)],
                                    reads=['kT', 'qT'], writes=spn[j % 2])
                        s_pair(0)
                        for j in range(NJ):
                            if j + 1 < NJ:
                                s_pair(j + 1)
                            S.op('act', lambda: A.activation(PT[j], spair[j % 2][:, 0:1024], AF.Exp, scale=128.0 ** -0.5),
                                 reads=spn[j % 2], writes=['PT%d' % j])
                            fns = []
                            for u in range(2):
                                kt = 2 * j + u
                                fns.append(lambda kt=kt, u=u: P.matmul(po, vsb[:, kt, :], PT[j][:, 512 * u:512 * (u + 1)],
                                                                       start=(kt == 0), stop=(kt == NT - 1)))
                            S.group('pe', fns, reads=['vsb', 'PT%d' % j], writes=['B6'])
                            if j <= 3:
                                blk_sig(n, j)
                            if j == 1 and n > 0:
                                blk_rs_mm(n - 1)
                            if j == 3 and n > 0:
                                blk_rs(n - 1)
                            if j == 5 and n > 0:
                                blk_epi2(n - 1)
                            if n + 1 < len(blocks):
                                if gate_pass:
                                    if j == 4:
                                        blk_setup(n + 1)
                                    if j >= 4:
                                        blk_zmm(n + 1, 4 * (j - 4), 4 * (j - 3))
                                else:
                                    if j == 0:
                                        blk_setup(n + 1)
                                    blk_zmm(n + 1, 2 * j, 2 * j + 2)
                            if bg and j >= 2:
                                S.op(*bg.pop(0))
                                if bg:
                                    S.op(*bg.pop(0))
                        blk_epi1(n)
                    blk_rs_mm(len(blocks) - 1)
                    blk_rs(len(blocks) - 1)
                    blk_epi2(len(blocks) - 1)
                    w_rel(kZ)
            while bg:
                S.op(*bg.pop(0))
            phase_fence(N_ATT + N_HI + N_MIX + N_ML)
            if stop_after == 'B1':
                break
            mlstm_phase(s_)
            phase_fence(N_ML + N_A + N_C)
            if stop_after == 'B2':
                break
            NXR = 5
            xres = [carve(1024 * r, [128, 256], F32) for r in range(NXR)]
            ysb = [carve(1024 * (NXR + r), [128, 256], F32) for r in range(3)]
            steps = [(nb8, T) for nb8 in range(8) for T in range(NT)]

            def c_load(k):
                nb8, T = steps[k]
                bx = k % NXR
                S.dma('sp', xres[bx], x[s_, T * 128:(T + 1) * 128, nb8 * 256:(nb8 + 1) * 256],
                      writes=['xres%d' % bx], sem='xres%d' % bx)
            for k0 in range(NXR - 1):
                c_load(k0)
            WO_ = None
            for k, (nb8, T) in enumerate(steps):
                if T == 0:
                    if WO_ is not None:
                        w_rel(kO_)
                    WO_, WOr_, kO_ = w_get("OA" if nb8 % 2 == 0 else "OB")
                cs_ = slice(nb8 * 256, (nb8 + 1) * 256)
                b = k % 2
                b3 = k % 3
                bx = k % NXR
                if k + NXR - 1 < len(steps):
                    c_load(k + NXR - 1)
                if s_ + 1 < nseq:
                    if k == 0:
                        phaseA_load(s_ + 1, 0)
                    if k % 8 == 4 and k // 8 + 1 < NT:
                        phaseA_load(s_ + 1, k // 8 + 1)
                    if k % 8 == 2 and k >= 8:
                        phaseA_xpose(s_ + 1, k // 8 - 1)
                S.group('pe', [(lambda kc=kc: P.matmul(pj[b][:, 0:256], mixT[:, kc, T * 128:(T + 1) * 128], WO_[:, kc, :],
                                                       start=(kc == 0), stop=(kc == KC - 1))) for kc in range(KC)],
                        reads=['mix%d' % T, WOr_], writes=['B%d' % b])
                S.op('dve', lambda: V.tensor_tensor(ysb[b3], pj[b][:, 0:256], xres[bx], ALU.add),
                     reads=['B%d' % b, 'xres%d' % bx], writes=['ysb%d' % b3])
                S.dma('sp', y[s_, T * 128:(T + 1) * 128, cs_], ysb[b3], reads=['ysb%d' % b3], sem='yst%d' % b3)
            if s_ + 1 < nseq:
                phaseA_xpose(s_ + 1, NT - 1)
            w_rel(kO_)
            phase_fence(N_A + N_C + N_ATT + N_HI + N_MIX)
        S.finish()
    return nc


_PROG = {}


def _as_np(a):
    return np.ascontiguousarray(np.asarray(a, dtype=np.float32))


def kernel(x_prompt, x_sample, norm_g, w_in, b_gates, q_norm_g, k_norm_g, mlstm_norm_g, w_out):
    x_prompt = np.asarray(x_prompt)
    x_sample = np.asarray(x_sample)
    seqs = [x_prompt[i] for i in range(x_prompt.shape[0])] + [x_sample[i] for i in range(x_sample.shape[0])]
    assert len(seqs) == N_CORES * SEQ_PER_CORE
    if 'nc' not in _PROG:
        _PROG['nc'] = build_program(SEQ_PER_CORE)
    nc = _PROG['nc']
    shared = dict(host_consts())
    shared.update({
        "w_in": _as_np(w_in)[0], "w_out": _as_np(w_out)[0],
        "norm_g": _as_np(norm_g).reshape(1, D), "b_gates": _as_np(b_gates).reshape(1, 16),
        "q_norm_g": _as_np(q_norm_g).reshape(1, 128), "k_norm_g": _as_np(k_norm_g).reshape(1, 128),
        "mlstm_norm_g": _as_np(mlstm_norm_g).reshape(1, 1024),
    })
    in_maps = []
    for c in range(N_CORES):
        m = dict(shared)
        m["x"] = np.ascontiguousarray(np.stack(seqs[c * SEQ_PER_CORE:(c + 1) * SEQ_PER_CORE]).astype(np.float32))
        in_maps.append(m)
    res = run_bass_kernel_spmd(nc, in_maps, core_ids=list(range(N_CORES)))
    ys = np.concatenate([np.asarray(res.results[c]["y"]) for c in range(N_CORES)], axis=0)
    nb = x_prompt.shape[0]
    return (np.ascontiguousarray(ys[:nb]).astype(np.float32), np.ascontiguousarray(ys[nb:]).astype(np.float32))
```

```python
import numpy as np
from contextlib import ExitStack
import ml_dtypes
import concourse.bass as bass
import concourse.mybir as mybir
from concourse.bass_utils import run_bass_kernel_spmd

F32 = mybir.dt.float32
BF16 = mybir.dt.bfloat16
ALU = mybir.AluOpType
AF = mybir.ActivationFunctionType
AX = mybir.AxisListType


class Sched:
    EPOCH = 30000

    def __init__(self, nc, es):
        self.nc = nc
        self.es = es
        self.E = {'pe': nc.tensor, 'act': nc.scalar, 'dve': nc.vector, 'pool': nc.gpsimd, 'sp': nc.sync}
        self.sems = {}
        self.cnt = {}
        self.epoch = {k: 0 for k in self.E}
        self.known = {k: {} for k in self.E}
        self.snap = {}
        self.lastw = {}
        self.readers = {}
        self.nwaits = 0
        self.nops = 0
        self.deferred = None

    def sem(self, name):
        if name not in self.sems:
            self.sems[name] = self.es.enter_context(self.nc.semaphore(name))
            self.cnt[name] = 0
        return self.sems[name]

    def sb(self, name, shape, dt):
        return self.es.enter_context(self.nc.sbuf_tensor(name, shape, dt))

    def ps(self, name, shape, dt):
        return self.es.enter_context(self.nc.psum_tensor(name, shape, dt))

    @staticmethod
    def _bankify(reads, writes):
        rb = [r for r in reads if len(r) == 2 and r[0] == 'B' and r[1].isdigit()]
        if not rb:
            return reads, writes
        return [r for r in reads if r not in rb], list(writes) + [r for r in rb if r not in writes]

    def _deps(self, reads, writes):
        need = {}
        for r in reads:
            t = self.lastw.get(r)
            if t is not None:
                need[t[0]] = max(need.get(t[0], 0), t[1])
        for w in writes:
            t = self.lastw.get(w)
            if t is not None:
                need[t[0]] = max(need.get(t[0], 0), t[1])
            for (s, v) in self.readers.get(w, {}).items():
                need[s] = max(need.get(s, 0), v)
        return need

    def _wait(self, eng, need):
        kn = self.known[eng]
        changed = False
        for s, v in need.items():
            if kn.get(s, 0) >= v:
                continue
            if eng == 'pe' and s.startswith('pe_e'):
                continue
            self.E[eng].wait_ge(self.sems[s], v)
            self.nwaits += 1
            if not changed:
                kn = dict(kn)
                changed = True
            kn[s] = v
            sn = self.snap.get((s, v))
            if sn is not None:
                for s2, v2 in sn.items():
                    if kn.get(s2, 0) < v2:
                        kn[s2] = v2
        if changed:
            self.known[eng] = kn

    def _commit(self, ticket, reads, writes):
        for w in writes:
            self.lastw[w] = ticket
            self.readers[w] = {}
        for r in reads:
            if r in writes:
                continue
            d = self.readers.setdefault(r, {})
            d[ticket[0]] = max(d.get(ticket[0], 0), ticket[1])

    def op(self, eng, fn, reads=(), writes=()):
        if self.deferred is not None:
            self.deferred.append((eng, fn, tuple(reads), tuple(writes)))
            return None
        reads, writes = self._bankify(reads, writes)
        self._wait(eng, self._deps(reads, writes))
        sname = "%s_e%d" % (eng, self.epoch[eng])
        sem = self.sem(sname)
        ins = fn()
        ins.then_inc(sem, 1)
        self.cnt[sname] += 1
        ticket = (sname, self.cnt[sname])
        self.snap[ticket] = self.known[eng]
        if self.cnt[sname] >= self.EPOCH:
            self.epoch[eng] += 1
        self._commit(ticket, reads, writes)
        self.nops += 1
        return ticket

    def group(self, eng, fns, reads=(), writes=()):
        reads, writes = self._bankify(reads, writes)
        self._wait(eng, self._deps(reads, writes))
        sname = "%s_e%d" % (eng, self.epoch[eng])
        sem = self.sem(sname)
        ins = None
        for fn in fns:
            ins = fn()
        ins.then_inc(sem, 1)
        self.cnt[sname] += 1
        ticket = (sname, self.cnt[sname])
        self.snap[ticket] = self.known[eng]
        if self.cnt[sname] >= self.EPOCH:
            self.epoch[eng] += 1
        self._commit(ticket, reads, writes)
        self.nops += len(fns)
        return ticket

    def fence(self, eng, fn, names):
        return self.op(eng, fn, reads=(), writes=list(names))

    def barrier(self):
        need = {s: c for s, c in self.cnt.items() if c > 0}
        for e in self.E:
            self._wait(e, need)

    def dma(self, eng, out, in_, reads=(), writes=(), sem='dma', **kw):
        return self.dmas(eng, [(out, in_)], reads, writes, sem, **kw)

    def dmas(self, eng, pairs, reads=(), writes=(), sem='dma', **kw):
        self._wait(eng, self._deps(reads, writes))
        s = self.sem(sem)
        for (o, i) in pairs:
            self.E[eng].dma_start(out=o, in_=i, **kw).then_inc(s, 16)
            self.cnt[sem] += 16
        ticket = (sem, self.cnt[sem])
        self.snap[ticket] = self.known[eng]
        self._commit(ticket, reads, writes)
        self.nops += len(pairs)
        return ticket

    def finish(self):
        need = {s: c for s, c in self.cnt.items() if c > 0}
        self._wait('sp', need)


D = 2048
SEQ = 2048
NT = SEQ // 128
KC = D // 128
DIN = 6672
C_AQ, C_AK, C_AV, C_AZ = 0, 1024, 1280, 1536
C_MQ, C_MK, C_MV, C_MO, C_MZ, C_G = 2560, 3072, 3584, 4608, 5632, 6656
EPS = 1e-6
M0 = -30000.0
N_CORES = 8
SEQ_PER_CORE = 3

ARENA_BYTES = 44160


def host_consts():
    bf = ml_dtypes.bfloat16
    c = {}
    i = np.arange(128)
    c["c_ident_bf"] = np.eye(128, dtype=np.float32).astype(bf)
    c["c_ones_bf"] = np.ones((128, 128), np.float32).astype(bf)
    c["c_maskf_bf"] = (i[:, None] <= i[None, :]).astype(np.float32).astype(bf)
    c["c_maskb_bf"] = (i[:, None] >= i[None, :]).astype(np.float32).astype(bf)
    f = np.zeros((128, 5, 128), np.float32)
    f[:, 0, :] = np.eye(128)
    f[:, 1, :] = 1.0
    f[:, 2, :] = (i[:, None] <= i[None, :])
    f[:, 3, :] = (i[:, None] >= i[None, :])
    f[:, 4, :] = (i[:, None] % 32 == 0)
    c["c_f32"] = f
    t = np.arange(SEQ)
    row = (t // 64).astype(np.float64)
    col = (t % 64).astype(np.float64)
    nf = 32
    inv = 1.0 / (10000.0 ** (np.arange(nf, dtype=np.float64) / nf))
    ang_r = row[:, None] * inv
    ang_c = col[:, None] * inv
    ang = np.concatenate([ang_r, ang_r, ang_c, ang_c], axis=-1)
    cos = np.cos(ang).astype(np.float32)
    sin = np.sin(ang).astype(np.float32)
    sgn = np.concatenate([-np.ones(32), np.ones(32), -np.ones(32), np.ones(32)]).astype(np.float32)
    tab = np.stack([cos, sin * sgn[None, :]], axis=1)
    c["c_rope"] = np.ascontiguousarray(tab.reshape(NT, 128, 2, 128))
    return c


def build_program(nseq=SEQ_PER_CORE, dbg=None, stop_after=None):
    nc = bass.Bass("TRN2", target_bir_lowering=False)

    def din(name, shape, dt=F32):
        return nc.dram_tensor(name, shape, dt, kind="ExternalInput").ap()

    x = din("x", [nseq, SEQ, D])
    w_in = din("w_in", [D, DIN])
    w_out = din("w_out", [D, D])
    norm_g = din("norm_g", [1, D])
    b_gates = din("b_gates", [1, 16])
    q_norm_g = din("q_norm_g", [1, 128])
    k_norm_g = din("k_norm_g", [1, 128])
    mlstm_norm_g = din("mlstm_norm_g", [1, 1024])
    c_ident_bf = din("c_ident_bf", [128, 128], BF16)
    c_ones_bf = din("c_ones_bf", [128, 128], BF16)
    c_maskf_bf = din("c_maskf_bf", [128, 128], BF16)
    c_maskb_bf = din("c_maskb_bf", [128, 128], BF16)
    c_f32 = din("c_f32", [128, 5, 128])
    c_rope = din("c_rope", [NT, 128, 2, 128])
    y = nc.dram_tensor("y", [nseq, SEQ, D], F32, kind="ExternalOutput").ap()
    dbg_out = {}
    if dbg:
        for name, shape in dbg.items():
            dbg_out[name] = nc.dram_tensor("dbg_" + name, shape, F32, kind="ExternalOutput").ap()

    w_in_v = w_in.rearrange("(kc p) c -> p kc c", p=128)
    w_out_v = w_out.rearrange("(kc p) c -> p kc c", p=128)

    with ExitStack() as es:
        S = Sched(nc, es)
        V, A, P = nc.vector, nc.scalar, nc.tensor

        hT = S.sb("hT", [128, KC, SEQ], BF16)
        mixT = S.sb("mixT", [128, KC, SEQ], BF16)
        WR = [S.sb("WR%d" % i, [128, KC, 256], BF16) for i in range(3)]
        arena = S.sb("arena", [128, ARENA_BYTES // 2], BF16)
        identb = S.sb("identb", [128, 128], BF16)
        onesb = S.sb("onesb", [128, 128], BF16)
        maskb_t = [S.sb("maskf", [128, 128], BF16), S.sb("maskb", [128, 128], BF16)]
        cf = S.sb("cf", [128, 5, 128], F32)
        gcol = S.sb("gcol", [128, KC], F32)
        gqk = S.sb("gqk", [128, 3, 128], F32)
        biasb = S.sb("biasb", [128, 16], F32)
        Wg = S.sb("Wg", [128, KC, 16], BF16)
        Graw = S.sb("Graw", [128, NT, 16], F32)
        gt = {n: S.sb("g_" + n, [128, NT, 8], F32) for n in
              ["gi", "lf", "bb", "gbc", "imx", "Mx", "lam", "eX", "thr", "tA"]}
        mE = S.sb("g_mE", [128, NT + 1, 8], F32)
        st = S.sb("stats", [128, 96], F32)
        gmcol = S.sb("gmcol", [128, 8], F32)

        PP = [S.ps("PP%d" % i, [128, 1024], F32) for i in range(4)]

        def carve(off, shape, dt):
            n = int(np.prod(shape[1:]))
            nb = n * (4 if dt == F32 else 2)
            assert off % 4 == 0 and off + nb <= ARENA_BYTES, (off, nb)
            v = arena[:, off // 2:(off + nb) // 2]
            if dt == F32:
                v = v.bitcast(F32)
            if len(shape) == 3:
                v = v.rearrange("p (a b) -> p a b", a=shape[1])
            elif len(shape) == 4:
                v = v.rearrange("p (a b c) -> p a b c", a=shape[1], b=shape[2])
            return v

        S.dmas('sp', [
            (identb[:], c_ident_bf[:]), (onesb[:], c_ones_bf[:]),
            (maskb_t[0][:], c_maskf_bf[:]), (maskb_t[1][:], c_maskb_bf[:]),
            (cf[:], c_f32[:]),
            (gcol[:], norm_g.rearrange("o (kc p) -> p (o kc)", p=128)),
            (gqk[:, 0, :], q_norm_g.partition_broadcast(128)),
            (gqk[:, 1, :], q_norm_g.partition_broadcast(128)),
            (gqk[:, 2, :], k_norm_g.partition_broadcast(128)),
            (biasb[:], b_gates.partition_broadcast(128)),
            (gmcol[:], mlstm_norm_g.rearrange("o (j p) -> p (o j)", p=128)),
        ], writes=['const'], sem='cst', allow_slow_non_contiguous=True)
        S.dma('pool', Wg[:], w_in_v[:, :, C_G:C_G + 16], writes=['Wg'], sem='cst2')
        ident_f = cf[:, 0, :]
        ones_f = cf[:, 1, :]
        tri_f = cf[:, 2, :]
        tri_b = cf[:, 3, :]
        sel_f = cf[:, 4, :]
        diag = carve(42240, [128, 128], F32)

        plan = []
        for s_ in range(nseq):
            for g in range(2):
                for a in range(2):
                    plan.append(("A", [('in', C_AQ + (4 * g + 2 * a) * 128, 256)]))
                    if a == 0:
                        plan.append(("B", [('in', C_AK + g * 128, 128), ('in', C_AV + g * 128, 128)]))
                    plan.append(("Z", [('in', C_AZ + (4 * g + 2 * a) * 128, 256)]))
            for h in range(4):
                plan.append(("QK", [('in', C_MQ + h * 128, 128), ('in', C_MK + h * 128, 128)]))
                plan.append(("V", [('in', C_MV + h * 256, 256)]))
                plan.append(("O", [('in', C_MO + h * 256, 256)]))
                plan.append(("MZ", [('in', C_MZ + h * 256, 256)]))
            for nb in range(4):
                plan.append(("OA", [('out', nb * 512, 256)]))
                plan.append(("OB", [('out', nb * 512 + 256, 256)]))
        wstate = {"issued": 0, "cur": 0, "rel": set()}

        def w_issue():
            while wstate["issued"] < len(plan):
                k = wstate["issued"]
                if k >= 3 and (k - 3) not in wstate["rel"]:
                    break
                if k > wstate["cur"] + 2:
                    break
                slot = k % 3
                pairs = []
                off = 0
                for (src, c0, ncol) in plan[k][1]:
                    sv = w_in_v if src == 'in' else w_out_v
                    pairs.append((WR[slot][:, :, off:off + ncol], sv[:, :, c0:c0 + ncol]))
                    off += ncol
                S.dmas('pool', pairs, writes=['WR%d' % slot], sem='wr%d' % slot)
                wstate["issued"] += 1

        def w_get(tag):
            k = wstate["cur"]
            assert plan[k][0] == tag, (plan[k][0], tag, k)
            w_issue()
            assert wstate["issued"] > k, ("weight ring deadlock", k, tag)
            wstate["cur"] += 1
            return WR[k % 3], 'WR%d' % (k % 3), k

        def w_rel(k):
            wstate["rel"].add(k)
            w_issue()

        def hres(t0, t1):
            return ['hT%d' % t for t in range(t0, t1)]

        pj = [PP[0][:, 0:512], PP[0][:, 512:1024]]
        PPb1 = PP[1].bitcast(BF16)
        ptr3 = PPb1[:, 0:384].rearrange("p (h e) -> p h e", h=3)
        ptr2 = [PPb1[:, 0:128], PPb1[:, 128:256]]
        pgt = PP[1][:, 512:528]
        pz = PP[1][:, 512:1024]
        pzq = [PP[1][:, 512 + 128 * i:640 + 128 * i] for i in range(4)]
        PZQ = ['B3']
        pgq = [PP[1][:, 128 * i:128 * (i + 1)] for i in range(4)]
        ps_ = [PP[2][:, 0:512], PP[2][:, 512:1024]]
        ptrm = [PP[0].bitcast(BF16)[:, 0:128], PP[0].bitcast(BF16)[:, 1024:1152]]
        psm = [PP[1][:, 0:128], PP[1][:, 512:640]]
        po = PP[3][:, 0:512]
        pr = PP[3][:, 512:1024]


        def gates_pipeline():
            G5 = Graw[:, :, :].rearrange("p t (d k h) -> p t d k h", d=2, k=2)
            B4 = biasb[:, :].rearrange("p (d k h) -> p d k h", d=2, k=2)
            gi, lf, bb, gbc, imx, Mx, lam, eX, thr, tA = (gt[n] for n in
                                                         ["gi", "lf", "bb", "gbc", "imx", "Mx", "lam", "eX", "thr", "tA"])

            def v4(t):
                return t[:, :, :].rearrange("p t (d h) -> p t d h", d=2)

            def f2(t):
                return t[:, :, :].rearrange("p t c -> p (t c)")

            S.op('dve', lambda: V.tensor_tensor(v4(gi), G5[:, :, :, 0, :],
                                                 B4[:, None, :, 0, :].to_broadcast([128, NT, 2, 4]), ALU.add),
                 reads=['Graw', 'const'], writes=['gi'])
            S.op('dve', lambda: V.tensor_tensor(v4(lf), G5[:, :, :, 1, :],
                                                 B4[:, None, :, 1, :].to_broadcast([128, NT, 2, 4]), ALU.add),
                 reads=['Graw', 'const'], writes=['lf'])
            S.op('act', lambda: A.activation(f2(lf), f2(lf), AF.Exp, scale=-1.0), reads=['lf'], writes=['lf'])
            S.op('act', lambda: A.activation(f2(lf), f2(lf), AF.Ln, bias=1.0), reads=['lf'], writes=['lf'])
            S.op('dve', lambda: V.tensor_scalar(f2(lf), f2(lf), -1.0, None, ALU.mult), reads=['lf'], writes=['lf'])
            S.op('pe', lambda: P.matmul(pgq[0], tri_f, f2(lf), start=True, stop=True), reads=['lf', 'const'], writes=['B2'])
            S.op('pe', lambda: P.matmul(pgq[1], tri_b, f2(lf), start=True, stop=True), reads=['lf', 'const'], writes=['B2'])
            S.op('pe', lambda: P.matmul(pgq[2], ones_f, f2(lf), start=True, stop=True), reads=['lf', 'const'], writes=['B2'])
            p0 = pgq[0].rearrange("p (t c) -> p t c", c=8)
            p1 = pgq[1].rearrange("p (t c) -> p t c", c=8)
            S.op('dve', lambda: V.tensor_copy(bb[:, :, 0:4], p0[:, :, 0:4]), reads=['B2'], writes=['bb'])
            S.op('dve', lambda: V.tensor_copy(bb[:, :, 4:8], p1[:, :, 4:8]), reads=['B2'], writes=['bb'])
            S.op('act', lambda: A.copy(f2(gbc), pgq[2]), reads=['B2'], writes=['gbc'])
            S.op('pe', lambda: P.matmul(pgq[3], f2(gi), ident_f, start=True, stop=True), reads=['gi', 'const'], writes=['B2'])
            S.op('dve', lambda: V.tensor_reduce(st[:, 56:57], pgq[3], AX.X, ALU.max), reads=['B2'], writes=['imcol'])
            S.op('dve', lambda: V.tensor_scalar(diag[:], ident_f, st[:, 56:57], None, ALU.mult),
                 reads=['imcol', 'const'], writes=['diag'])
            S.op('pe', lambda: P.matmul(pgq[0], ones_f, diag[:], start=True, stop=True), reads=['diag', 'const'], writes=['B2'])
            S.op('dve', lambda: V.tensor_copy(f2(imx), pgq[0]), reads=['B2'], writes=['imx'])
            S.op('dve', lambda: V.memset(mE[:], M0), writes=['mE'])
            for dh in range(4):
                S.op('dve', lambda dh=dh: V.tensor_tensor_scan(mE[:, 1:NT + 1, dh], gbc[:, :, dh], imx[:, :, dh], M0,
                                                                ALU.add, ALU.max),
                     reads=['gbc', 'imx', 'mE'], writes=['mE'])
            for dh in range(4, 8):
                S.op('dve', lambda dh=dh: V.tensor_tensor_scan(mE[:, 0:NT, dh][:, ::-1], gbc[:, :, dh][:, ::-1],
                                                                imx[:, :, dh][:, ::-1], M0, ALU.add, ALU.max),
                     reads=['gbc', 'imx', 'mE'], writes=['mE'])
            S.op('dve', lambda: V.tensor_tensor(Mx[:, :, 0:4], mE[:, 0:NT, 0:4], imx[:, :, 0:4], ALU.max),
                 reads=['mE', 'imx'], writes=['Mx'])
            S.op('dve', lambda: V.tensor_tensor(Mx[:, :, 4:8], mE[:, 1:NT + 1, 4:8], imx[:, :, 4:8], ALU.max),
                 reads=['mE', 'imx'], writes=['Mx'])
            S.op('dve', lambda: V.tensor_tensor(tA[:, :, :], gbc[:, :, :], Mx[:, :, :], ALU.add), reads=['gbc', 'Mx'], writes=['tA'])
            S.op('dve', lambda: V.tensor_tensor(tA[:, 0:NT - 1, 0:4], tA[:, 0:NT - 1, 0:4], Mx[:, 1:NT, 0:4], ALU.subtract),
                 reads=['tA', 'Mx'], writes=['tA'])
            S.op('dve', lambda: V.tensor_tensor(tA[:, 1:NT, 4:8], tA[:, 1:NT, 4:8], Mx[:, 0:NT - 1, 4:8], ALU.subtract),
                 reads=['tA', 'Mx'], writes=['tA'])
            S.op('act', lambda: A.activation(f2(lam), f2(tA), AF.Exp), reads=['tA'], writes=['lam'])
            S.op('dve', lambda: V.tensor_tensor(tA[:, :, :], gi[:, :, :], bb[:, :, :], ALU.subtract), reads=['gi', 'bb'], writes=['tA'])
            S.op('dve', lambda: V.tensor_tensor(tA[:, :, :], tA[:, :, :], Mx[:, :, :], ALU.subtract), reads=['tA', 'Mx'], writes=['tA'])
            S.op('act', lambda: A.activation(f2(eX), f2(tA), AF.Exp), reads=['tA'], writes=['eX'])
            S.op('dve', lambda: V.tensor_tensor(tA[:, :, :], bb[:, :, :], Mx[:, :, :], ALU.add), reads=['bb', 'Mx'], writes=['tA'])
            S.op('act', lambda: A.activation(f2(thr), f2(tA), AF.Exp, scale=-1.0), reads=['tA'], writes=['thr'])

        def mlstm_phase(s_):
            qTm = carve(0, [128, SEQ], BF16)
            kTm = carve(4096, [128, SEQ], BF16)
            vext = carve(8192, [128, NT, 257], BF16)
            hsum = carve(16416, [128, NT, 256], F32)
            U = [carve(32800, [128, 258], F32), carve(33832, [128, 258], F32)]
            Cs = [carve(34864, [128, 258], BF16), carve(35380, [128, 258], BF16)]
            Pm = [[carve(35896, [128, 128], BF16), carve(42048, [128, 128], BF16), carve(43072, [128, 128], BF16)],
                  [carve(36152, [128, 128], BF16), carve(42304, [128, 128], BF16), carve(43328, [128, 128], BF16)]]
            kx = [[carve(36408, [128, 128], BF16), carve(42560, [128, 128], BF16), carve(43584, [128, 128], BF16)],
                  [carve(36664, [128, 128], BF16), carve(42816, [128, 128], BF16), carve(43840, [128, 128], BF16)]]
            sozb = [carve(36920, [128, 512], F32), carve(38968, [128, 512], F32), carve(32800, [128, 512], F32)]
            obb = [carve(41016, [128, 256], BF16), carve(41528, [128, 256], BF16), carve(34864, [128, 256], BF16)]
            htmp = [carve(36920, [128, 256], F32), carve(37944, [128, 256], F32)]
            ALIAS = ['U0', 'U1', 'Cs0', 'soz2', 'soz2z', 'soz2o', 'ob2', 'soz0', 'soz0z', 'soz0o', 'htmp0', 'htmp1']
            pjf = [PP[0][:, 0:512], PP[0][:, 512:1024], PP[2][:, 0:512]]
            pjfn = ['B0', 'B1', 'B4']
            eX, thr, lam = gt['eX'], gt['thr'], gt['lam']
            S.op('dve', lambda: V.memset(vext[:, :, 256:257], 1.0), writes=['vones'])
            for h in range(4):
                WQK, WQKr, kQK = w_get("QK")
                WV, WVr, kV = w_get("V")
                for tb in range(4):
                    ts = slice(tb * 512, (tb + 1) * 512)
                    S.group('pe', [(lambda kc=kc: P.matmul(pj[0], WQK[:, kc, 0:128], hT[:, kc, ts],
                                                           start=(kc == 0), stop=(kc == KC - 1))) for kc in range(KC)],
                            reads=hres(4 * tb, 4 * tb + 4) + [WQKr], writes=['B0'])
                    S.op('act', lambda: A.activation(qTm[:, ts], pj[0], AF.Copy, scale=128.0 ** -0.5),
                         reads=['B0'], writes=['qTm'])
                    S.group('pe', [(lambda kc=kc: P.matmul(pj[1], WQK[:, kc, 128:256], hT[:, kc, ts],
                                                           start=(kc == 0), stop=(kc == KC - 1))) for kc in range(KC)],
                            reads=hres(4 * tb, 4 * tb + 4) + [WQKr], writes=['B1'])
                    S.op('dve', lambda: V.tensor_copy(kTm[:, ts], pj[1]), reads=['B1'], writes=['kTm'])
                for T in range(NT):
                    b = T % 2
                    S.group('pe', [(lambda kc=kc: P.matmul(pj[b][:, 0:256], hT[:, kc, T * 128:(T + 1) * 128], WV[:, kc, :],
                                                           start=(kc == 0), stop=(kc == KC - 1))) for kc in range(KC)],
                            reads=['hT%d' % T, WVr], writes=['B%d' % b])
                    if b == 0:
                        S.op('act', lambda: A.copy(vext[:, T, 0:256], pj[b][:, 0:256]), reads=['B%d' % b], writes=['vext'])
                    else:
                        S.op('dve', lambda: V.tensor_copy(vext[:, T, 0:256], pj[b][:, 0:256]), reads=['B%d' % b], writes=['vext'])
                w_rel(kQK)
                w_rel(kV)
                def ch(i, dr):
                    c = i if dr == 0 else NT - 1 - i
                    return c, (c - 1 if dr == 0 else c + 1), dr * 4 + h, slice(c * 128, (c + 1) * 128)

                def emit_front(i):
                    for dr in range(2):
                        c, cprev, dh, cs = ch(i, dr)
                        eXc = eX[:, c, dh:dh + 1]
                        S.op('pe', lambda: P.transpose(ptrm[dr], kTm[:, cs], identb[:]),
                             reads=['kTm', 'const'], writes=['B%d' % dr])
                        S.op('pe', lambda: P.matmul(psm[dr], kTm[:, cs], qTm[:, cs], start=True, stop=True),
                             reads=['kTm', 'qTm'], writes=['B%d' % (2 + dr)])
                        S.op('act', lambda: A.activation(kx[dr][i % 3], ptrm[dr], AF.Copy, scale=eXc),
                             reads=['B%d' % dr, 'eX'], writes=['kx%d%d' % (dr, i % 3)])
                        S.op('dve', lambda: V.scalar_tensor_tensor(Pm[dr][i % 3], psm[dr], eXc, maskb_t[dr][:], ALU.mult, ALU.mult),
                             reads=['B%d' % (2 + dr), 'eX', 'const'], writes=['Pm%d%d' % (dr, i % 3)])

                def emit_dC(i):
                    for dr in range(2):
                        c, cprev, dh, cs = ch(i, dr)
                        pCn = 'B6' if dr == 0 else 'B7'
                        pdC = (po if dr == 0 else pr)[:, 0:257]
                        S.op('pe', lambda: P.matmul(pdC, kx[dr][i % 3], vext[:, c, :], start=True, stop=True),
                             reads=['kx%d%d' % (dr, i % 3), 'vext', 'vones'], writes=[pCn])
                    for dr in range(2):
                        c, cprev, dh, cs = ch(i, dr)
                        pCn = 'B6' if dr == 0 else 'B7'
                        pdC = (po if dr == 0 else pr)[:, 0:257]
                        if i == 0:
                            S.op('dve', lambda: V.tensor_copy(U[dr][:, 0:257], pdC), reads=[pCn], writes=['U%d' % dr])
                        else:
                            lamp = lam[:, cprev, dh:dh + 1]
                            S.op('dve', lambda: V.scalar_tensor_tensor(U[dr][:, 0:257], U[dr][:, 0:257], lamp, pdC,
                                                                        ALU.mult, ALU.add),
                                 reads=['U%d' % dr, pCn, 'lam'], writes=['U%d' % dr])

                def emit_cs(i):
                    for dr in range(2):
                        c, cprev, dh, cs = ch(i, dr)
                        if i < NT - 1:
                            lamc = lam[:, c, dh:dh + 1]
                            if i >= NT // 2 and dr == 0:
                                S.op('act', lambda: A.activation(Cs[dr][:, 0:257], U[dr][:, 0:257], AF.Copy, scale=lamc),
                                     reads=['U%d' % dr, 'lam'], writes=['Cs%d' % dr])
                            else:
                                S.op('pool', lambda: nc.gpsimd.tensor_tensor(Cs[dr][:, 0:257], U[dr][:, 0:257], lamc.to_broadcast([128, 257]), ALU.mult),
                                     reads=['U%d' % dr, 'lam'], writes=['Cs%d' % dr])

                def emit_N(i):
                    for dr in range(2):
                        c, cprev, dh, cs = ch(i, dr)
                        pNn = 'B%d' % (4 + dr)
                        pN = ps_[dr][:, 0:257]
                        fns = [lambda: P.matmul(pN, Pm[dr][i % 3], vext[:, c, :], start=True, stop=(i == 0))]
                        if i > 0:
                            fns.append(lambda: P.matmul(pN, qTm[:, cs], Cs[dr][:, 0:257], start=False, stop=True))
                        S.group('pe', fns, reads=['Pm%d%d' % (dr, i % 3), 'vext', 'vones', 'qTm', 'Cs%d' % dr], writes=[pNn])
                    emit_cs(i)
                    for dr in range(2):
                        c, cprev, dh, cs = ch(i, dr)
                        thrc = thr[:, c, dh:dh + 1]
                        pNn = 'B%d' % (4 + dr)
                        pN = ps_[dr][:, 0:257]
                        dd = st[:, 58 + dr:59 + dr]
                        rec = st[:, 60 + dr:61 + dr]
                        S.op('dve', lambda: V.tensor_scalar(dd, pN[:, 256:257], thrc, None, ALU.max),
                             reads=[pNn, 'thr'], writes=['hd%d' % dr])
                        S.op('dve', lambda: V.scalar_tensor_tensor(dd, pN[:, 256:257], -1.0, dd, ALU.mult, ALU.max),
                             reads=[pNn, 'hd%d' % dr], writes=['hd%d' % dr])
                        S.op('dve', lambda: V.reciprocal(rec, dd), reads=['hd%d' % dr], writes=['hr%d' % dr])
                        if i < NT // 2:
                            S.op('act', lambda: A.activation(hsum[:, c, :], pN[:, 0:256], AF.Copy, scale=rec),
                                 reads=[pNn, 'hr%d' % dr], writes=['hsum%d' % c])
                        else:
                            S.op('act', lambda: A.activation(htmp[dr], pN[:, 0:256], AF.Copy, scale=rec),
                                 reads=[pNn, 'hr%d' % dr], writes=['htmp%d' % dr])
                            S.op('pool', lambda: nc.gpsimd.tensor_tensor(hsum[:, c, :], hsum[:, c, :], htmp[dr], ALU.add),
                                 reads=['htmp%d' % dr, 'hsum%d' % c], writes=['hsum%d' % c])

                S.fence('dve', lambda: V.memset(st[:, 95:96], 0.0), ALIAS)
                emit_front(0)
                emit_front(1)
                for i in range(NT):
                    emit_dC(i)
                    if i + 2 < NT:
                        emit_front(i + 2)
                    emit_N(i)
                S.fence('dve', lambda: V.memset(st[:, 95:96], 0.0), ALIAS)
                WO, WOr, kO = w_get("O")
                WZ, WZr, kMZ = w_get("MZ")

                def fin_proj(T):
                    b = T % 3
                    Ts = slice(T * 128, (T + 1) * 128)
                    fns = [(lambda kc=kc: P.matmul(pjf[b][:, 0:256], hT[:, kc, Ts], WO[:, kc, :],
                                                   start=(kc == 0), stop=(kc == KC - 1))) for kc in range(KC)]
                    fns += [(lambda kc=kc: P.matmul(pjf[b][:, 256:512], hT[:, kc, Ts], WZ[:, kc, :],
                                                    start=(kc == 0), stop=(kc == KC - 1))) for kc in range(KC)]
                    S.group('pe', fns, reads=['hT%d' % T, WOr, WZr], writes=[pjfn[b]])

                def fin_one(T):
                    b = T % 3
                    soz = sozb[b]
                    sn, on = 'soz%d' % b, 'ob%d' % b
                    c_ss, c_rl, c_rs = 64 + b, 68 + b, 72 + b
                    S.op('act', lambda: A.activation(soz, pjf[b], AF.Exp, scale=-1.0), reads=[pjfn[b]],
                         writes=[sn, sn + 'z', sn + 'o'])
                    S.op('act', lambda: A.activation(soz, soz, AF.Ln, bias=1.0), reads=[sn], writes=[sn])
                    S.op('act', lambda: A.activation(soz, soz, AF.Exp, scale=-1.0), reads=[sn], writes=[sn])
                    S.op('dve', lambda: V.tensor_tensor(soz[:, 256:512], soz[:, 256:512], pjf[b][:, 256:512], ALU.mult),
                         reads=[sn, pjfn[b]], writes=[sn + 'z'])
                    S.op('pool', lambda: nc.gpsimd.tensor_tensor(soz[:, 0:256], soz[:, 0:256], hsum[:, T, :], ALU.mult),
                         reads=[sn, 'hsum%d' % T], writes=[sn + 'o'])
                    S.op('act', lambda: A.activation(obb[b], soz[:, 0:256], AF.Square, accum_out=st[:, c_ss:c_ss + 1]),
                         reads=[sn + 'o'], writes=[on, 'ssm%d' % b])
                    S.op('act', lambda: A.activation(st[:, c_rl:c_rl + 1], st[:, c_ss:c_ss + 1], AF.Ln, bias=EPS, scale=1.0 / 256),
                         reads=['ssm%d' % b], writes=['rlm%d' % b])
                    S.op('act', lambda: A.activation(st[:, c_rs:c_rs + 1], st[:, c_rl:c_rl + 1], AF.Exp, scale=-0.5),
                         reads=['rlm%d' % b], writes=['rsm%d' % b])

                def fin_two(T):
                    b = T % 3
                    Ts = slice(T * 128, (T + 1) * 128)
                    soz, ob = sozb[b], obb[b]
                    sn, on = 'soz%d' % b, 'ob%d' % b
                    c_rs = 72 + b
                    S.op('dve', lambda: V.scalar_tensor_tensor(ob, soz[:, 0:256], st[:, c_rs:c_rs + 1], soz[:, 256:512],
                                                                ALU.mult, ALU.mult),
                         reads=[sn + 'o', sn + 'z', 'rsm%d' % b], writes=[on])
                    S.group('pe', [(lambda k=k: P.transpose(ptr2[k], ob[:, k * 128:(k + 1) * 128], identb[:])) for k in range(2)],
                            reads=[on, 'const'], writes=['B2'])
                    for k in range(2):
                        S.op('dve', lambda: V.tensor_scalar(mixT[:, 8 + 2 * h + k, Ts], ptr2[k],
                                                             gmcol[:, 2 * h + k:2 * h + k + 1], None, ALU.mult),
                             reads=['B2', 'const'], writes=['mix%d' % T])

                fin_proj(0)
                fin_proj(1)
                fin_one(0)
                for T in range(NT):
                    if T + 2 < NT:
                        fin_proj(T + 2)
                    if T + 1 < NT:
                        fin_one(T + 1)
                    fin_two(T)
                w_rel(kO)
                w_rel(kMZ)

        xinA = [carve(8192, [128, D], F32), carve(16384, [128, D], F32), carve(32768, [128, D], F32)]
        xnA = [carve(24576, [128, D], BF16), carve(28672, [128, D], BF16)]
        PA = [PP[2].bitcast(BF16).rearrange("p (k e) -> p k e", k=KC),
              PP[3].bitcast(BF16).rearrange("p (k e) -> p k e", k=KC)]
        PAR = [['B4', 'B5'], ['B6', 'B7']]

        def phaseA_load(sq_, T, bufs=None):
            b = T % 2
            xin_l, xn = (xinA, xnA) if bufs is None else (bufs[0], bufs[1])
            bx = T % len(xin_l)
            xin = {b: xin_l[bx]}
            S.dma('sp', xin[b], x[sq_, T * 128:(T + 1) * 128, :], writes=['xin%d' % bx], sem='xin%d' % bx)
            S.op('act', lambda: A.activation(xn[b], xin[b], AF.Square, accum_out=st[:, b:b + 1]),
                 reads=['xin%d' % bx], writes=['xn%d' % b, 'ss%d' % b])
            S.op('act', lambda: A.activation(st[:, 2 + b:3 + b], st[:, b:b + 1], AF.Ln, bias=EPS, scale=1.0 / D),
                 reads=['ss%d' % b], writes=['rl%d' % b])
            S.op('act', lambda: A.activation(st[:, 4 + b:5 + b], st[:, 2 + b:3 + b], AF.Exp, scale=-0.5),
                 reads=['rl%d' % b], writes=['rs%d' % b])
            S.op('act', lambda: A.activation(xn[b], xin[b], AF.Copy, scale=st[:, 4 + b:5 + b]),
                 reads=['xin%d' % bx, 'rs%d' % b], writes=['xn%d' % b])

        def phaseA_xpose(sq_, T, bufs=None):
            b = T % 2
            xn = xnA if bufs is None else bufs[1]
            pa, par = (PA[b], PAR[b]) if bufs is None else (PA[1], PAR[1])
            S.group('pe', [(lambda kc=kc: P.transpose(pa[:, kc, :], xn[b][:, kc * 128:(kc + 1) * 128], identb[:]))
                           for kc in range(KC)],
                    reads=['xn%d' % b, 'const'], writes=par)
            S.op('dve', lambda: V.tensor_tensor(hT[:, :, T * 128:(T + 1) * 128], pa,
                                                 gcol[:, :].unsqueeze(2).to_broadcast([128, KC, 128]), ALU.mult),
                 reads=par + ['const'], writes=['hT%d' % T])

        def phaseA_tile(sq_, T):
            phaseA_load(sq_, T)
            phaseA_xpose(sq_, T)


        N_A = ['xin0', 'xin1', 'xin2', 'xn0', 'xn1']
        N_C = ['xres%d' % i for i in range(3)] + ['ysb%d' % i for i in range(5)]
        N_ATT = ([p + str(r) for r in range(4) for p in ('xg', 't1', 't2a', 't2b', 'qr', 'rope')] + ['sqj', 'diag', 'prs0', 'prs1', 'selb', 'rhl0', 'rhl1']
                 + ['PT%d' % r for r in range(2, 8)])
        N_HI = ['qT', 'kT', 'vsb', 'PT0', 'PT1', 'szb0', 'szb1', 'rinv0', 'rinv1', 'ot0', 'ot1']
        N_MIX = ['mix%d' % t for t in range(NT)]
        N_ML = (['qTm', 'kTm', 'vext', 'vones'] + ['hsum%d' % t for t in range(NT)]
                + ['U0', 'U1', 'Cs0', 'Cs1', 'htmp0', 'htmp1']
                + ['Pm%d%d' % (a_, b_) for a_ in range(2) for b_ in range(3)]
                + ['kx%d%d' % (a_, b_) for a_ in range(2) for b_ in range(3)]
                + ['soz%d%s' % (i, sfx_) for i in range(3) for sfx_ in ('', 'z', 'o')] + ['ob%d' % i for i in range(3)])

        def phase_fence(names=None):
            if names is None:
                names = N_A + N_C + N_ATT + N_HI + N_MIX + N_ML
            S.fence('dve', lambda: V.memset(st[:, 94:95], 0.0), names)

        mx_lo = mixT[:, 0:8, :].rearrange("p k t -> p (k t)")
        A0 = ([mx_lo[:, 0:4096].bitcast(F32), mx_lo[:, 4096:8192].bitcast(F32)],
              [mx_lo[:, 8192:10240], mx_lo[:, 10240:12288]])

        for s_ in range(nseq):
            if stop_after == 'A':
                break
            mx_hi = mixT[:, 8:16, :].rearrange("p k t -> p (k t)")

            def carve2(off, shape, dt):
                n = int(np.prod(shape[1:]))
                nb = n * (4 if dt == F32 else 2)
                assert off % 4 == 0 and off + nb <= 32768
                v = mx_hi[:, off // 2:(off + nb) // 2]
                if dt == F32:
                    v = v.bitcast(F32)
                if len(shape) == 3:
                    v = v.rearrange("p (a b) -> p a b", a=shape[1])
                return v
            qT = carve2(0, [128, 2, SEQ], BF16)
            kT = carve2(8192, [128, SEQ], BF16)
            vsb = carve2(12288, [128, NT, 128], BF16)
            PT = [carve2(16384, [128, 1024], BF16), carve2(30720, [128, 1024], BF16)]
            szb = [carve2(18432, [128, 512], F32), carve2(20480, [128, 512], F32)]
            rinv = [carve2(22528, [128, 512], F32), carve2(24576, [128, 512], F32)]
            ot = [carve2(26624, [128, 512], F32), carve2(28672, [128, 512], F32)]
            RA = 4
            xg = [carve(5376 * r, [128, 3, 128], F32) for r in range(RA)]
            t1 = [carve(5376 * r + 1536, [128, 3, 128], F32) for r in range(RA)]
            t2 = [carve(5376 * r + 3072, [128, 3, 128], F32) for r in range(RA)]
            qr = [carve(5376 * r + 4608, [128, 3, 128], BF16) for r in range(RA)]
            sqj = carve(5376 * RA, [128, 128], BF16)
            rope = [carve(5376 * RA + 256 + 1024 * r, [128, 2, 128], F32) for r in range(RA)]
            PT = PT + [carve(25856 + 2048 * r, [128, 1024], BF16) for r in range(6)]
            prs = [carve(38144, [128, 512], F32)] * 2
            selb = carve(42752, [128, 128], BF16)
            rhl = [carve(40192, [128, 1024], BF16)] * 2
            S.op('act', lambda: A.copy(selb, sel_f), reads=['const'], writes=['selb'])
            pjr = [PP[0][:, 0:512], PP[0][:, 512:1024], PP[2][:, 0:512], PP[2][:, 512:1024]]
            pjrn = ['B0', 'B1', 'B4', 'B5']
            nblk = 0
            bg = []
            for g in range(2):
                for a in range(2):
                    nh = 3 if a == 0 else 2
                    WA, WAr, kA = w_get("A")
                    if a == 0:
                        WB, WBr, kB = w_get("B")
                    gate_pass = (g == 0 and a == 0)

                    def st_proj(T):
                        pjb = T % RA
                        pjn = pjrn[pjb]
                        S.dma('sp', rope[pjb], c_rope[T], writes=['rope%d' % pjb], sem='rope%d' % pjb)
                        fns = [(lambda kc=kc: P.matmul(pjr[pjb][:, 0:256], hT[:, kc, T * 128:(T + 1) * 128], WA[:, kc, :],
                                                       start=(kc == 0), stop=(kc == KC - 1))) for kc in range(KC)]
                        rd = ['hT%d' % T, WAr]
                        if a == 0:
                            fns += [(lambda kc=kc: P.matmul(pjr[pjb][:, 256:512], hT[:, kc, T * 128:(T + 1) * 128], WB[:, kc, :],
                                                            start=(kc == 0), stop=(kc == KC - 1))) for kc in range(KC)]
                            rd.append(WBr)
                        S.group('pe', fns, reads=rd, writes=[pjn])
                        if gate_pass:
                            pgr = PP[1][:, 512 + 16 * pjb:528 + 16 * pjb]
                            S.group('pe', [(lambda kc=kc: P.matmul(pgr, hT[:, kc, T * 128:(T + 1) * 128], Wg[:, kc, :],
                                                                   start=(kc == 0), stop=(kc == KC - 1))) for kc in range(KC)],
                                    reads=['hT%d' % T, 'Wg'], writes=['B3'])

                    def st_one(T):
                        pb = T % RA
                        rb = pb
                        pjn = pjrn[pb]
                        sfx = '%d' % pb
                        pv = pjr[pb][:, 0:nh * 128].rearrange("p (h e) -> p h e", h=nh)
                        c_ss, c_rl, c_rs = 8 + 4 * pb, 24 + 4 * pb, 40 + 4 * pb
                        if gate_pass:
                            S.op('dve', lambda: V.tensor_copy(Graw[:, T, :], PP[1][:, 512 + 16 * pb:528 + 16 * pb]),
                                 reads=['B3'], writes=['Graw'])
                        for h in range(nh):
                            S.op('act', lambda: A.activation(sqj, pv[:, h, :], AF.Square, accum_out=st[:, c_ss + h:c_ss + h + 1]),
                                 reads=[pjn], writes=['sqj', 'ssq' + sfx + str(h)])
                        gsel = gqk[:, 0:3, :] if a == 0 else gqk[:, 0:2, :]
                        S.op('dve', lambda: V.tensor_tensor(xg[pb][:, 0:nh, :], pv, gsel, ALU.mult),
                             reads=[pjn, 'const'], writes=['xg' + sfx])
                        if a == 0:
                            S.op('act', lambda: A.copy(vsb[:, T, :], pjr[pb][:, 384:512]), reads=[pjn], writes=['vsb'])
                        S.op('act', lambda: A.activation(st[:, c_rl:c_rl + nh], st[:, c_ss:c_ss + nh], AF.Ln, bias=EPS, scale=1.0 / 128),
                             reads=['ssq' + sfx + str(h) for h in range(nh)], writes=['rlq' + sfx])
                        S.op('act', lambda: A.activation(st[:, c_rs:c_rs + nh], st[:, c_rl:c_rl + nh], AF.Exp, scale=-0.5),
                             reads=['rlq' + sfx], writes=['rsq' + sfx])
                        x5 = xg[pb][:, 0:nh, :].rearrange("p h (r f e) -> p h r f e", r=2, f=2)
                        t5 = t2[pb][:, 0:nh, :].rearrange("p h (r f e) -> p h r f e", r=2, f=2)
                        s4 = rope[rb][:, 1, :].rearrange("p (r f e) -> p r f e", r=2, f=2)
                        S.op('pool', lambda: nc.gpsimd.tensor_tensor(t5[:, :, :, 0, :], x5[:, :, :, 1, :],
                                                                      s4[:, None, :, 0, :].to_broadcast([128, nh, 2, 32]), ALU.mult),
                             reads=['xg' + sfx, 'rope%d' % rb], writes=['t2a' + sfx])
                        S.op('pool', lambda: nc.gpsimd.tensor_tensor(t5[:, :, :, 1, :], x5[:, :, :, 0, :],
                                                                      s4[:, None, :, 1, :].to_broadcast([128, nh, 2, 32]), ALU.mult),
                             reads=['xg' + sfx, 'rope%d' % rb], writes=['t2b' + sfx])
                        S.op('dve', lambda: V.tensor_tensor(t1[pb][:, 0:nh, :], xg[pb][:, 0:nh, :],
                                                             rope[rb][:, 0:1, :].to_broadcast([128, nh, 128]), ALU.mult),
                             reads=['xg' + sfx, 'rope%d' % rb], writes=['t1' + sfx])
                        S.op('dve', lambda: V.tensor_tensor(t1[pb][:, 0:nh, :], t1[pb][:, 0:nh, :], t2[pb][:, 0:nh, :], ALU.add),
                             reads=['t1' + sfx, 't2a' + sfx, 't2b' + sfx], writes=['t1' + sfx])
                        S.op('pool', lambda: nc.gpsimd.tensor_tensor(
                            qr[pb][:, 0:nh, :], t1[pb][:, 0:nh, :],
                            st[:, c_rs:c_rs + nh].unsqueeze(2).to_broadcast([128, nh, 128]), ALU.mult),
                             reads=['t1' + sfx, 'rsq' + sfx], writes=['qr' + sfx])

                    def st_two(T):
                        pb = T % RA
                        sfx = '%d' % pb
                        S.group('pe', [(lambda h=h: P.transpose(ptr3[:, h, :], qr[pb][:, h, :], identb[:])) for h in range(nh)],
                                reads=['qr' + sfx, 'const'], writes=['B2'])
                        S.op('dve', lambda: V.tensor_copy(qT[:, :, T * 128:(T + 1) * 128], ptr3[:, 0:2, :]),
                             reads=['B2'], writes=['qT'])
                        if a == 0:
                            S.op('dve', lambda: V.tensor_copy(kT[:, T * 128:(T + 1) * 128], ptr3[:, 2, :]), reads=['B2'], writes=['kT'])

                    LA = RA - 1
                    if s_ == 0 and gate_pass:
                        for t in range(NT + 5):
                            if t < NT:
                                phaseA_load(0, t, A0)
                            if 0 <= t - 1 < NT:
                                phaseA_xpose(0, t - 1, A0)
                            if 0 <= t - 2 < NT:
                                st_proj(t - 2)
                            if 0 <= t - 4 < NT:
                                st_one(t - 4)
                            if 0 <= t - 5 < NT:
                                st_two(t - 5)
                        phase_fence(N_A + N_MIX)
                    else:
                        for T in range(LA):
                            st_proj(T)
                        st_one(0)
                        for T in range(NT):
                            if T + LA < NT:
                                st_proj(T + LA)
                            if T + 1 < NT:
                                st_one(T + 1)
                            st_two(T)
                    w_rel(kA)
                    if a == 0:
                        w_rel(kB)
                    if gate_pass:
                        S.deferred = []
                        gates_pipeline()
                        bg = S.deferred
                        S.deferred = None
                    WZ, WZr, kZ = w_get("Z")
                    blocks = [(hh, qb) for hh in range(2) for qb in range(4)]
                    binfo = {}
                    if gate_pass:
                        pzb = [PP[1][:, 512:1024], PP[1][:, 512:1024]]
                        pzbn = ['B3', 'B3']
                    else:
                        pzb = [PP[1][:, 512:1024], PP[1][:, 0:512]]
                        pzbn = ['B3', 'B2']
                    spair = [PP[2], PP[0]]
                    spn = [['B4', 'B5'], ['B0', 'B1']]

                    def blk_setup(n):
                        nonlocal nblk
                        hh, qb = blocks[n]
                        bp = nblk % 2
                        nblk += 1
                        binfo[n] = dict(hh=hh, qb=qb, qs=slice(qb * 512, (qb + 1) * 512), bp=bp, head=4 * g + 2 * a + hh)

                    def blk_zmm(n, k0, k1):
                        bi = binfo[n]
                        hh, qb, qs = bi['hh'], bi['qb'], bi['qs']
                        pzc, pzn = pzb[bi['bp']], pzbn[bi['bp']]
                        S.group('pe', [(lambda kc=kc: P.matmul(pzc, WZ[:, kc, hh * 128:(hh + 1) * 128], hT[:, kc, qs],
                                                               start=(kc == 0), stop=(kc == KC - 1))) for kc in range(k0, k1)],
                                reads=hres(4 * qb, 4 * qb + 4) + [WZr], writes=[pzn])

                    def blk_sig(n, step):
                        bp = binfo[n]['bp']
                        szc, szn = szb[bp], 'szb%d' % bp
                        pzc, pzn = pzb[bp], pzbn[bp]
                        if step == 0:
                            S.op('act', lambda: A.activation(szc, pzc, AF.Exp, scale=-1.0), reads=[pzn], writes=[szn])
                        elif step == 1:
                            S.op('act', lambda: A.activation(szc, szc, AF.Ln, bias=1.0), reads=[szn], writes=[szn])
                        elif step == 2:
                            S.op('act', lambda: A.activation(szc, szc, AF.Exp, scale=-1.0), reads=[szn], writes=[szn])
                        else:
                            S.op('dve', lambda: V.tensor_tensor(szc, szc, pzc, ALU.mult), reads=[szn, pzn], writes=[szn])

                    def blk_epi1(n):
                        bp = binfo[n]['bp']
                        fns = []
                        for kt in range(NT):
                            gq = kt % 4
                            fns.append(lambda kt=kt, gq=gq: P.matmul(pr[32 * gq:32 * gq + 32, :], onesb[:, 0:32],
                                                                     PT[kt // 2][:, 512 * (kt % 2):512 * (kt % 2 + 1)],
                                                                     start=(kt < 4), stop=(kt >= NT - 4), tile_position=(0, 32 * gq)))
                        S.group('pe', fns, reads=['const'] + ['PT%d' % r for r in range(8)], writes=['B7'])
                        S.op('dve', lambda: V.tensor_copy(ot[bp], po), reads=['B6'], writes=['ot%d' % bp])
                        S.op('dve', lambda: V.tensor_copy(prs[bp], pr), reads=['B7'], writes=['prs0'])
                        S.op('dve', lambda: V.tensor_copy(rhl[bp][:, 0:512], prs[bp]), reads=['prs0'], writes=['rhl0'])
                        S.op('dve', lambda: V.tensor_tensor(rhl[bp][:, 512:1024], prs[bp], rhl[bp][:, 0:512], ALU.subtract),
                             reads=['prs0', 'rhl0'], writes=['rhl0'])

                    def blk_rs_mm(n):
                        bp = binfo[n]['bp']
                        S.op('pe', lambda: P.matmul(pr, selb, rhl[bp][:, 0:512], start=True, stop=False),
                             reads=['selb', 'rhl0'], writes=['B7'])
                        S.op('pe', lambda: P.matmul(pr, selb, rhl[bp][:, 512:1024], start=False, stop=True),
                             reads=['selb', 'rhl0'], writes=['B7'])

                    def blk_rs(n):
                        bp = binfo[n]['bp']
                        S.op('act', lambda: A.activation(rinv[bp], pr, AF.Ln), reads=['B7'], writes=['rinv%d' % bp])

                    def blk_epi2(n):
                        bi = binfo[n]
                        qb, qs, bp, head = bi['qb'], bi['qs'], bi['bp'], bi['head']
                        szc, ric, otc = szb[bp], rinv[bp], ot[bp]
                        szn, rin, otn = 'szb%d' % bp, 'rinv%d' % bp, 'ot%d' % bp
                        S.op('act', lambda: A.activation(ric, ric, AF.Exp, scale=-1.0), reads=[rin], writes=[rin])
                        S.op('dve', lambda: V.tensor_tensor(otc, otc, ric, ALU.mult), reads=[otn, rin], writes=[otn])
                        S.op('pool', lambda: nc.gpsimd.tensor_tensor(mixT[:, head, qs], otc, szc, ALU.mult),
                             reads=[otn, szn], writes=['mix%d' % t for t in range(4 * qb, 4 * qb + 4)])

                    NJ = NT // 2
                    blk_setup(0)
                    blk_zmm(0, 0, KC)
                    for n in range(len(blocks)):
                        bi = binfo[n]
                        hh, qs = bi['hh'], bi['qs']

                        def s_pair(j):
                            sb_ = spair[j % 2]
                            S.group('pe', [(lambda u=u: P.matmul(sb_[:, 512 * u:512 * (u + 1)],
                                                                 kT[:, (2 * j + u) * 128:(2 * j + u + 1) * 128], qT[:, hh, qs],
                                                                 start=True, stop=True)) for u in range(2)],
                                    reads=['kT', 'qT'], writes=spn[j % 2])
                        s_pair(0)
                        for j in range(NJ):
                            if j + 1 < NJ:
                                s_pair(j + 1)
                            S.op('act', lambda: A.activation(PT[j], spair[j % 2][:, 0:1024], AF.Exp, scale=128.0 ** -0.5),
                                 reads=spn[j % 2], writes=['PT%d' % j])
                            fns = []
                            for u in range(2):
                                kt = 2 * j + u
                                fns.append(lambda kt=kt, u=u: P.matmul(po, vsb[:, kt, :], PT[j][:, 512 * u:512 * (u + 1)],
                                                                       start=(kt == 0), stop=(kt == NT - 1)))
                            S.group('pe', fns, reads=['vsb', 'PT%d' % j], writes=['B6'])
                            if j <= 3:
                                blk_sig(n, j)
                            if j == 1 and n > 0:
                                blk_rs_mm(n - 1)
                            if j == 3 and n > 0:
                                blk_rs(n - 1)
                            if j == 5 and n > 0:
                                blk_epi2(n - 1)
                            if n + 1 < len(blocks):
                                if gate_pass:
                                    if j == 4:
                                        blk_setup(n + 1)
                                    if j >= 4:
                                        blk_zmm(n + 1, 4 * (j - 4), 4 * (j - 3))
                                else:
                                    if j == 0:
                                        blk_setup(n + 1)
                                    blk_zmm(n + 1, 2 * j, 2 * j + 2)
                            if bg and j >= 2:
                                S.op(*bg.pop(0))
                                if bg:
                                    S.op(*bg.pop(0))
                        blk_epi1(n)
                    blk_rs_mm(len(blocks) - 1)
                    blk_rs(len(blocks) - 1)
                    blk_epi2(len(blocks) - 1)
                    w_rel(kZ)
            while bg:
                S.op(*bg.pop(0))
            phase_fence(N_ATT + N_HI + N_MIX + N_ML)
            if stop_after == 'B1':
                break
            mlstm_phase(s_)
            phase_fence(N_ML + N_A + N_C)
            if stop_after == 'B2':
                break
            xres = [carve(0, [128, 256], F32), carve(1024, [128, 256], F32), carve(2048, [128, 256], F32)]
            NYS = 5
            ysb = [carve(3072 + 1024 * r, [128, 256], F32) for r in range(NYS)]
            steps = [(nb8, T) for nb8 in range(8) for T in range(NT)]

            def c_load(k):
                nb8, T = steps[k]
                b3 = k % 3
                S.dma('sp', xres[b3], x[s_, T * 128:(T + 1) * 128, nb8 * 256:(nb8 + 1) * 256],
                      writes=['xres%d' % b3], sem='xres%d' % b3)
            c_load(0)
            c_load(1)
            WO_ = None
            for k, (nb8, T) in enumerate(steps):
                if T == 0:
                    if WO_ is not None:
                        w_rel(kO_)
                    WO_, WOr_, kO_ = w_get("OA" if nb8 % 2 == 0 else "OB")
                cs_ = slice(nb8 * 256, (nb8 + 1) * 256)
                b = k % 2
                b3 = k % 3
                if k + 2 < len(steps):
                    c_load(k + 2)
                if s_ + 1 < nseq:
                    if k == 0:
                        phaseA_load(s_ + 1, 0)
                    if k % 8 == 4 and k // 8 + 1 < NT:
                        phaseA_load(s_ + 1, k // 8 + 1)
                    if k % 8 == 2 and k >= 8:
                        phaseA_xpose(s_ + 1, k // 8 - 1)
                S.group('pe', [(lambda kc=kc: P.matmul(pj[b][:, 0:256], mixT[:, kc, T * 128:(T + 1) * 128], WO_[:, kc, :],
                                                       start=(kc == 0), stop=(kc == KC - 1))) for kc in range(KC)],
                        reads=['mix%d' % T, WOr_], writes=['B%d' % b])
                by = k % NYS
                S.op('dve', lambda: V.tensor_tensor(ysb[by], pj[b][:, 0:256], xres[b3], ALU.add),
                     reads=['B%d' % b, 'xres%d' % b3], writes=['ysb%d' % by])
                S.dma('sp', y[s_, T * 128:(T + 1) * 128, cs_], ysb[by], reads=['ysb%d' % by], sem='yst%d' % by)
            if s_ + 1 < nseq:
                phaseA_xpose(s_ + 1, NT - 1)
            w_rel(kO_)
            phase_fence(N_A + N_C + N_ATT + N_HI + N_MIX)
        S.finish()
    return nc


_PROG = {}


def _as_np(a):
    return np.ascontiguousarray(np.asarray(a, dtype=np.float32))


def kernel(x_prompt, x_sample, norm_g, w_in, b_gates, q_norm_g, k_norm_g, mlstm_norm_g, w_out):
    x_prompt = np.asarray(x_prompt)
    x_sample = np.asarray(x_sample)
    seqs = [x_prompt[i] for i in range(x_prompt.shape[0])] + [x_sample[i] for i in range(x_sample.shape[0])]
    assert len(seqs) == N_CORES * SEQ_PER_CORE
    if 'nc' not in _PROG:
        _PROG['nc'] = build_program(SEQ_PER_CORE)
    nc = _PROG['nc']
    shared = dict(host_consts())
    shared.update({
        "w_in": _as_np(w_in)[0], "w_out": _as_np(w_out)[0],
        "norm_g": _as_np(norm_g).reshape(1, D), "b_gates": _as_np(b_gates).reshape(1, 16),
        "q_norm_g": _as_np(q_norm_g).reshape(1, 128), "k_norm_g": _as_np(k_norm_g).reshape(1, 128),
        "mlstm_norm_g": _as_np(mlstm_norm_g).reshape(1, 1024),
    })
    in_maps = []
    for c in range(N_CORES):
        m = dict(shared)
        m["x"] = np.ascontiguousarray(np.stack(seqs[c * SEQ_PER_CORE:(c + 1) * SEQ_PER_CORE]).astype(np.float32))
        in_maps.append(m)
    res = run_bass_kernel_spmd(nc, in_maps, core_ids=list(range(N_CORES)))
    ys = np.concatenate([np.asarray(res.results[c]["y"]) for c in range(N_CORES)], axis=0)
    nb = x_prompt.shape[0]
    return (np.ascontiguousarray(ys[:nb]).astype(np.float32), np.ascontiguousarray(ys[nb:]).astype(np.float32))
```
